# Optimizing a Trainium2 kernel written in Bass

```python
import math
import jax, jax.numpy as jnp
from jax import lax
import numpy as np

D_MODEL = 2048
BATCH = 2
SEQ = 8192
DEPTH = 1

N_META = 16
POOL_WIDTH = 1024
POOL_GROUPS = 4
POOL_WINDOWS = (2, 4, 8, 16)
POOL_GC = POOL_WIDTH // POOL_GROUPS
CONV_WIDTH = 1024
CONV_K = 3
N_BRANCH = 2
IN_COLS = POOL_WIDTH + 3 * CONV_WIDTH + N_BRANCH * D_MODEL
PEER_HEADS = 8
PEER_QDIM = 256
PEER_HALF = PEER_QDIM // 2
N_KEYS = 128
N_EXPERTS = N_KEYS * N_KEYS
PEER_TOPK = 16
PEER_BLOCK = 128
DN_ALPHA = (2.0 * DEPTH) ** 0.25
DN_BETA = (8.0 * DEPTH) ** -0.25
LN_EPS = 1e-5

kernel_name = "hybrid_pool_shortconv_peer_deepnorm_block"


def _layer_norm(x, g, b):
    xf = x.astype(jnp.float32)
    mu = jnp.mean(xf, axis=-1, keepdims=True)
    var = jnp.mean(jnp.square(xf - mu), axis=-1, keepdims=True)
    y = (xf - mu) * lax.rsqrt(var + LN_EPS) * g.astype(jnp.float32) + b.astype(jnp.float32)
    return y.astype(x.dtype)


def _causal_multiscale_pool(a):
    bsz, t_len, _ = a.shape
    af = a.astype(jnp.float32).reshape(bsz, t_len, POOL_GROUPS, POOL_GC)
    cs = jnp.cumsum(af, axis=1)
    pos = jnp.arange(t_len)
    outs = []
    for g, w in enumerate(POOL_WINDOWS):
        csg = cs[:, :, g]
        lag = jnp.pad(csg, ((0, 0), (w, 0), (0, 0)))[:, :t_len]
        cnt = jnp.minimum(pos + 1, w).astype(jnp.float32)[None, :, None]
        outs.append((csg - lag) / cnt - af[:, :, g])
    return jnp.stack(outs, axis=2).astype(a.dtype)


def _causal_depthwise_conv(z, w, b):
    y = lax.conv_general_dilated(
        z, w, window_strides=(1,), padding=[(CONV_K - 1, 0)],
        dimension_numbers=("NWC", "WIO", "NWC"), feature_group_count=z.shape[-1])
    return y + b


def _peer(xf, w_q, keys_1, keys_2, expert_u, expert_v):
    n = xf.shape[0]
    n_pad = (-n) % PEER_BLOCK
    xb_all = jnp.pad(xf, ((0, n_pad), (0, 0))).reshape(-1, PEER_BLOCK, D_MODEL)

    def block(xb):
        q = (xb @ w_q).reshape(PEER_BLOCK, PEER_HEADS, 2, PEER_HALF)
        s1 = jnp.einsum("thc,hnc->thn", q[:, :, 0], keys_1)
        s2 = jnp.einsum("thc,hnc->thn", q[:, :, 1], keys_2)
        v1, i1 = lax.top_k(s1, PEER_TOPK)
        v2, i2 = lax.top_k(s2, PEER_TOPK)
        n_cand = PEER_TOPK * PEER_TOPK
        cand_s = (v1[..., :, None] + v2[..., None, :]).reshape(PEER_BLOCK, PEER_HEADS, n_cand)
        cand_i = (i1[..., :, None] * N_KEYS + i2[..., None, :]).reshape(PEER_BLOCK, PEER_HEADS, n_cand)
        top_s, sel = lax.top_k(cand_s, PEER_TOPK)
        idx = jnp.take_along_axis(cand_i, sel, axis=-1).reshape(PEER_BLOCK, PEER_HEADS * PEER_TOPK)
        gate = jax.nn.softmax(top_s.astype(jnp.float32), axis=-1).reshape(PEER_BLOCK, PEER_HEADS * PEER_TOPK)
        u = expert_u[idx]
        act = jax.nn.gelu(jnp.einsum("td,tkd->tk", xb, u).astype(jnp.float32), approximate=False)
        coef = (gate * act).astype(xb.dtype)
        return jnp.einsum("tk,tkd->td", coef, expert_v[idx])

    return lax.map(block, xb_all).reshape(-1, D_MODEL)[:n]


def setup_inputs(seed: int = 0) -> dict:
    key = jax.random.key(seed)
    ks = jax.random.split(key, 24)
    f32 = jnp.float32
    L = DEPTH

    def nrm(k, shape, scale):
        return jax.random.normal(k, shape, f32) * scale

    return {
        "x": nrm(ks[0], (BATCH, SEQ, D_MODEL), 1.0),
        "meta_tokens": nrm(ks[1], (N_META, D_MODEL), 1.0),
        "ln_in_g": 1.0 + nrm(ks[2], (D_MODEL,), 0.02),
        "ln_in_b": nrm(ks[3], (D_MODEL,), 0.02),
        "w_in": nrm(ks[4], (L, D_MODEL, IN_COLS), D_MODEL ** -0.5),
        "pool_w": nrm(ks[5], (L, POOL_GROUPS, POOL_GC, POOL_GC), POOL_GC ** -0.5),
        "pool_scale": 1.0 + nrm(ks[6], (L, POOL_WIDTH), 0.02),
        "pool_proj": nrm(ks[7], (L, POOL_WIDTH, D_MODEL), POOL_WIDTH ** -0.5),
        "conv_w": nrm(ks[8], (L, CONV_K, 1, CONV_WIDTH), CONV_K ** -0.5),
        "conv_b": nrm(ks[9], (L, CONV_WIDTH), 0.02),
        "conv_proj": nrm(ks[10], (L, CONV_WIDTH, D_MODEL), CONV_WIDTH ** -0.5),
        "w_out": nrm(ks[11], (L, D_MODEL, D_MODEL), DN_BETA * D_MODEL ** -0.5),
        "ln1_g": 1.0 + nrm(ks[12], (L, D_MODEL), 0.02),
        "ln1_b": nrm(ks[13], (L, D_MODEL), 0.02),
        "peer_wq": nrm(ks[14], (L, D_MODEL, PEER_HEADS * PEER_QDIM), D_MODEL ** -0.5),
        "peer_keys_1": nrm(ks[15], (L, PEER_HEADS, N_KEYS, PEER_HALF), PEER_HALF ** -0.5),
        "peer_keys_2": nrm(ks[16], (L, PEER_HEADS, N_KEYS, PEER_HALF), PEER_HALF ** -0.5),
        "expert_u": nrm(ks[17], (L, N_EXPERTS, D_MODEL), D_MODEL ** -0.5),
        "expert_v": nrm(ks[18], (L, N_EXPERTS, D_MODEL), DN_BETA * PEER_HEADS ** -0.5),
        "ln2_g": 1.0 + nrm(ks[19], (L, D_MODEL), 0.02),
        "ln2_b": nrm(ks[20], (L, D_MODEL), 0.02),
    }


def reference(x, meta_tokens, ln_in_g, ln_in_b, w_in, pool_w, pool_scale, pool_proj,
              conv_w, conv_b, conv_proj, w_out, ln1_g, ln1_b, peer_wq, peer_keys_1,
              peer_keys_2, expert_u, expert_v, ln2_g, ln2_b):
    bsz = x.shape[0]
    meta = jnp.broadcast_to(meta_tokens.astype(x.dtype)[None], (bsz, N_META, D_MODEL))
    h = jnp.concatenate([meta, x], axis=1)
    h = _layer_norm(h, ln_in_g, ln_in_b)
    t_len = h.shape[1]
    c1 = POOL_WIDTH
    c2 = c1 + CONV_WIDTH
    c3 = c2 + CONV_WIDTH
    c4 = c3 + CONV_WIDTH

    for l in range(DEPTH):
        proj = h @ w_in[l]
        a = proj[..., :c1]
        zh, zb, zc = proj[..., c1:c2], proj[..., c2:c3], proj[..., c3:c4]
        gates = jax.nn.sigmoid(proj[..., c4:].astype(jnp.float32)).astype(h.dtype)
        gates = gates.reshape(bsz, t_len, N_BRANCH, D_MODEL)

        pooled = _causal_multiscale_pool(a)
        ya = jnp.einsum("btgc,gcd->btgd", pooled, pool_w[l]).reshape(bsz, t_len, POOL_WIDTH)
        ya = (ya * pool_scale[l]) @ pool_proj[l]

        yb = zb * _causal_depthwise_conv(zc * zh, conv_w[l], conv_b[l])
        yb = yb @ conv_proj[l]

        mixed = (gates[:, :, 0] * ya + gates[:, :, 1] * yb) @ w_out[l]
        h = _layer_norm(DN_ALPHA * h + mixed, ln1_g[l], ln1_b[l])

        y = _peer(h.reshape(-1, D_MODEL), peer_wq[l], peer_keys_1[l], peer_keys_2[l],
                  expert_u[l], expert_v[l]).reshape(bsz, t_len, D_MODEL)
        h = _layer_norm(DN_ALPHA * h + y, ln2_g[l], ln2_b[l])

    return h[:, N_META:]
```

```python
import numpy as np
import concourse.bass as bass
import concourse.mybir as mybir
from concourse.bass_utils import run_bass_kernel_spmd
from contextlib import ExitStack

F32 = mybir.dt.float32
BF16 = mybir.dt.bfloat16
U32 = mybir.dt.uint32
ALU = mybir.AluOpType
AF = mybir.ActivationFunctionType
AX = mybir.AxisListType

D = 2048
NCORE = 8
TOK = 2048
TBK = 512
NBLK = 4
EXT = TBK + 16
ALPHA = 2.0 ** 0.25
EPS = 1e-5
NEG = -1.0e30


class Buf:
    __slots__ = ("name", "w", "r")

    def __init__(self, name=""):
        self.name = name
        self.w = {}
        self.r = {}


class DSem:
    __slots__ = ("sem", "cnt")

    def __init__(self, sem):
        self.sem = sem
        self.cnt = 0


class Slot:
    def __init__(self, S, name, t):
        self.t = t
        self.buf = Buf(name)
        self.name = name
        self.S = S
        self._din = None
        self._dout = None

    @property
    def din(self):
        if self._din is None:
            self._din = self.S.dsem(self.name + "_i")
        return self._din

    @property
    def dout(self):
        if self._dout is None:
            self._dout = self.S.dsem(self.name + "_o")
        return self._dout


class Sched:
    EPOCH = 20000

    def __init__(self, nc, es):
        self.nc = nc
        self.es = es
        self.eng = {"pe": nc.tensor, "act": nc.scalar, "dve": nc.vector,
                    "pool": nc.gpsimd, "sp": nc.sync}
        self.sem = {}
        self.cnt = {}
        self.allsem = []
        self.dsems = []
        self.waited = {k: {} for k in self.eng}
        self.nsem = 0
        for k in self.eng:
            self._new_epoch(k)

    def _new_epoch(self, k):
        self.nsem += 1
        self.sem[k] = self.es.enter_context(self.nc.semaphore(f"e_{k}_{self.nsem}"))
        self.cnt[k] = 0
        self.allsem.append([self.sem[k], 0])

    def dsem(self, name):
        self.nsem += 1
        d = DSem(self.es.enter_context(self.nc.semaphore("d_" + name)))
        self.dsems.append(d)
        return d

    def _wait(self, k, deps):
        E = self.eng[k]
        wd = self.waited[k]
        for sem, val in deps.items():
            if wd.get(sem, 0) >= val:
                continue
            E.wait_ge(sem, val)
            wd[sem] = val

    def _deps(self, k, reads, writes):
        deps = {}
        for b in reads:
            for s, v in b.w.items():
                if deps.get(s, 0) < v:
                    deps[s] = v
        for b in writes:
            for s, v in b.w.items():
                if deps.get(s, 0) < v:
                    deps[s] = v
            for s, v in b.r.items():
                if deps.get(s, 0) < v:
                    deps[s] = v
        if k == "pe":
            deps.pop(self.sem["pe"], None)
        return deps

    def op(self, k, fn, reads=(), writes=()):
        self._wait(k, self._deps(k, reads, writes))
        ins = fn(self.eng[k])
        if self.cnt[k] >= self.EPOCH:
            self._new_epoch(k)
        self.cnt[k] += 1
        ins.then_inc(self.sem[k], 1)
        s, v = self.sem[k], self.cnt[k]
        for e in self.allsem:
            if e[0] is s:
                e[1] = v
        for b in reads:
            if b.r.get(s, 0) < v:
                b.r[s] = v
        for b in writes:
            b.w = {s: v}
            b.r = {}
        return ins

    def dma(self, q, out, in_, ds, reads=(), writes=(), **kw):
        self._wait(q, self._deps(q, reads, writes))
        ins = self.eng[q].dma_start(out=out, in_=in_, **kw)
        ins.then_inc(ds.sem, 16)
        ds.cnt += 16
        for b in reads:
            if b.r.get(ds.sem, 0) < ds.cnt:
                b.r[ds.sem] = ds.cnt
        for b in writes:
            b.w = {ds.sem: ds.cnt}
            b.r = {}
        return ins

    def load(self, q, slot, dst_ap, src_ap, reads=(), **kw):
        return self.dma(q, dst_ap, src_ap, slot.din, reads=reads, writes=[slot.buf], **kw)

    def store(self, q, dst_ap, slot, src_ap, dram_bufs=(), **kw):
        return self.dma(q, dst_ap, src_ap, slot.dout, reads=[slot.buf], writes=list(dram_bufs), **kw)

    def barrier(self, engines=("pe", "act", "dve", "pool", "sp")):
        deps = {}
        for s, v in self.allsem:
            if v > 0:
                deps[s] = v
        for d in self.dsems:
            if d.cnt > 0:
                deps[d.sem] = d.cnt
        for k in engines:
            dd = dict(deps)
            if k == "pe":
                dd.pop(self.sem["pe"], None)
            self._wait(k, dd)


def build_program(stage=99, dbg=False):
    nc = bass.Bass("TRN2", target_bir_lowering=False)

    def din(name, shape, dt=F32):
        return nc.dram_tensor(name, shape, dt, kind="ExternalInput").ap()

    def dscr(name, shape, dt=F32):
        return nc.dram_tensor(name, shape, dt, kind=("ExternalOutput" if dbg else "Internal")).ap()

    x_in = din("x_in", [NBLK, EXT, D])
    lnv = din("lnv", [6, 128, D])
    sm_d = din("sm", [128, 40])
    cst_d = din("cst", [128, 128 + 128 + 2048])
    win = din("win", [64, 128, 2048])
    pw_d = din("pw", [128, 2048])
    pcp = din("pcp", [16, 128, 2048])
    wo_d = din("wo", [4, 128, 8192])
    wq_d = din("wq", [16, 128, 2048])
    kt_d = din("kt", [128, 2048])
    ut_d = din("ut", [128, 128, 2048])
    vv_d = din("vv", [128, 128, 2048])
    out_d = nc.dram_tensor("out", [TOK, D], F32, kind="ExternalOutput").ap()

    h_s = dscr("h_s", [TOK, D])
    h1_s = dscr("h1_s", [TOK, D])
    h1T_s = dscr("h1T_s", [NBLK, 128, 16 * TBK], BF16)
    G_s = dscr("G_s", [16, 128, 16384], BF16)
    NE = 96
    GE_s = dscr("GE_s", [NE, 128, TOK], BF16)

    def is_early(c):
        return (c % 8) < 6

    def eidx(c):
        return (c // 8) * 6 + (c % 8)

    Bh_s = [Buf(f"h_s{i}") for i in range(16)]
    Bh1_s = [Buf(f"h1_s{i}") for i in range(16)]
    Bh1T_s = [Buf(f"h1T_s{i}") for i in range(NBLK)]
    BG_s = [Buf(f"G_s{i}") for i in range(16)]
    BGE = [[Buf(f"GE{c}_{k}") for k in range(NBLK)] for c in range(NE)]
    Bout = [Buf(f"out{i}") for i in range(16)]

    with ExitStack() as es:
        S = Sched(nc, es)

        def sb(stack, name, shape, dt=F32):
            return stack.enter_context(nc.sbuf_tensor("s_" + name, shape, dt))

        psA = es.enter_context(nc.psum_tensor("psA", [128, 2048], F32))
        psB = es.enter_context(nc.psum_tensor("psB", [128, 2048], F32))
        PB = [Buf(f"bank{i}") for i in range(8)]

        def bank(i):
            t = psA if i < 4 else psB
            return t[:, (i % 4) * 512:(i % 4 + 1) * 512]

        bank_rr = [0]

        def next_bank():
            i = bank_rr[0]
            bank_rr[0] = (i + 1) % 8
            return i

        dbg_outs = {}

        def dump(name, slot, ap, shape, dt=F32):
            if not dbg or name in dbg_outs:
                return
            d = nc.dram_tensor("dbg_" + name, shape, dt, kind="ExternalOutput").ap()
            dbg_outs[name] = d
            S.store("sp", d, slot, ap, [Buf("dbg_" + name)])

        cst = sb(es, "cst", [128, 128 + 128 + 2048])
        smt = sb(es, "smt", [128, 40])
        Bcst = Buf("cst")
        dc = S.dsem("const")
        dc1 = S.dsem("const1")
        dc2 = S.dsem("const2")
        S.dma("sp", cst[:], cst_d[:, :], dc, writes=[Bcst])
        S.dma("sp", smt[:], sm_d[:, :], dc, writes=[Bcst])
        Bcst.w = {dc.sem: dc.cnt}
        ident = cst[:, 0:128]
        iota128 = cst[:, 128:256]
        iota16 = cst[:, 256:256 + 2048]

        def layer_norm(pr, T1, T2, sms, g_ap, b_ap, Bgb):
            st = sms.t
            stats = st[0:pr, 0:24]
            mv = st[0:pr, 24:26]
            sd = st[0:pr, 26:27]
            rstd = st[0:pr, 27:28]
            nb = st[0:pr, 28:29]
            z = T1.t[0:pr, :]
            xn = T2.t[0:pr, :]
            for c in range(4):
                S.op("dve", lambda e, c=c: e.bn_stats(out=stats[:, c * 6:(c + 1) * 6],
                                                      in_=z[:, c * 512:(c + 1) * 512]),
                     reads=[T1.buf], writes=[sms.buf])
            S.op("dve", lambda e: e.bn_aggr(out=mv, in_=stats), reads=[sms.buf], writes=[sms.buf])
            S.op("dve", lambda e: e.tensor_scalar(out=sd, in0=mv[:, 1:2], scalar1=EPS, scalar2=None,
                                                  op0=ALU.add), reads=[sms.buf], writes=[sms.buf])
            S.op("act", lambda e: e.activation(out=sd, in_=sd, func=AF.Sqrt),
                 reads=[sms.buf], writes=[sms.buf])
            S.op("dve", lambda e: e.reciprocal(out=rstd, in_=sd), reads=[sms.buf], writes=[sms.buf])
            S.op("dve", lambda e: e.scalar_tensor_tensor(out=nb, in0=mv[:, 0:1], scalar=-1.0, in1=rstd,
                                                         op0=ALU.mult, op1=ALU.mult),
                 reads=[sms.buf], writes=[sms.buf])
            S.op("act", lambda e: e.activation(out=xn, in_=z, func=AF.Identity, scale=rstd, bias=nb),
                 reads=[T1.buf, sms.buf], writes=[T2.buf])
            S.op("dve", lambda e: e.tensor_tensor(out=z, in0=xn, in1=g_ap[0:pr, :], op=ALU.mult),
                 reads=[T2.buf, Bgb], writes=[T1.buf])
            S.op("dve", lambda e: e.tensor_tensor(out=xn, in0=z, in1=b_ap[0:pr, :], op=ALU.add),
                 reads=[T1.buf, Bgb], writes=[T2.buf])

        def transpose_rows(pr, T2, dst3, col0, dst_buf):
            for q in range(4):
                bi = next_bank()
                bk = bank(bi)
                for j in range(4):
                    dk = q * 4 + j
                    S.op("pe", lambda e, dk=dk, j=j: e.transpose(
                        out=bk[:, j * 128:j * 128 + pr], in_=T2.t[0:pr, dk * 128:(dk + 1) * 128],
                        identity=ident[0:pr, 0:pr]),
                        reads=[T2.buf, Bcst], writes=[PB[bi]])
                src = bk.rearrange("p (j c) -> p j c", j=4)[:, :, 0:pr]
                S.op("act", lambda e, q=q, src=src: e.activation(
                    out=dst3[:, q * 4:(q + 1) * 4, col0:col0 + pr], in_=src, func=AF.Copy),
                    reads=[PB[bi]], writes=[dst_buf])

        with ExitStack() as ph:
            lnt = sb(ph, "lnt", [128, 4, D])
            Bln = Buf("lnt")
            for i in range(4):
                S.dma("sp", lnt[:, i, :], lnv[i], dc1, writes=[Bln])
            Bln.w = {dc1.sem: dc1.cnt}
            pwt = sb(ph, "pwt", [128, 2048], BF16)
            Spw = Slot(S, "pwt", pwt)
            S.load("pool", Spw, pwt[:], pw_d[:, :])
            pw4 = pwt[:].rearrange("p (a c o) -> p a c o", a=8, c=2)

            LT = [Slot(S, f"lt{i}", sb(ph, f"lt{i}", [128, D])) for i in range(4)]
            LS = [Slot(S, f"ls{i}", sb(ph, f"ls{i}", [128, 32])) for i in range(2)]
            hTs = [Slot(S, f"hT{i}", sb(ph, f"hT{i}", [128, 16, EXT], BF16)) for i in range(2)]
            aext = Slot(S, "aext", sb(ph, "aext", [128, EXT]))
            tmpA = Slot(S, "tmpA", sb(ph, "tmpA", [128, EXT]))
            tmpB = Slot(S, "tmpB", sb(ph, "tmpB", [128, EXT]))
            pooledT = Slot(S, "pooledT", sb(ph, "pooledT", [128, 8, TBK], BF16))
            ya1T = Slot(S, "ya1T", sb(ph, "ya1T", [128, 8, TBK], BF16))
            yb1T = pooledT
            mT = Slot(S, "mT", sb(ph, "mT", [128, 16, TBK], BF16))
            sg0 = Slot(S, "sg0", sb(ph, "sg0", [128, TBK]))
            sg1 = Slot(S, "sg1", sb(ph, "sg1", [128, TBK]))
            NW = 5
            WR = [Slot(S, f"wr{i}", sb(ph, f"wr{i}", [128, 2048], BF16)) for i in range(NW)]
            WO = [Slot(S, f"wo{i}", sb(ph, f"wo{i}", [128, 8192], BF16)) for i in range(2)]
            wr_i = [0]

            def wload(src_ap):
                s = WR[wr_i[0] % NW]
                wr_i[0] += 1
                S.load("pool", s, s.t[:], src_ap)
                return s

            wo_i = [0]
            pending_tr = []

            def phase_a_ln(k, ti):
                pr = 128 if ti < 4 else 16
                r0 = ti * 128
                T1, T2, sms = LT[(ti % 2) * 2], LT[(ti % 2) * 2 + 1], LS[ti % 2]
                S.load("sp", T1, T1.t[0:pr, :], x_in[k, r0:r0 + pr, :])
                layer_norm(pr, T1, T2, sms, lnt[:, 0, :], lnt[:, 1, :], Bln)
                if ti < 4:
                    gt = k * 4 + ti
                    S.store("sp", h_s[gt * 128:(gt + 1) * 128, :], T2, T2.t[:, :], [Bh_s[gt]])

            def phase_a_tr(k, ti):
                pr = 128 if ti < 4 else 16
                T2 = LT[(ti % 2) * 2 + 1]
                hTn = hTs[k % 2]
                transpose_rows(pr, T2, hTn.t, ti * 128, hTn.buf)

            tail_items = []
            for k in range(NBLK):
                hT = hTs[k % 2]
                if k == 0:
                    for ti in range(5):
                        phase_a_ln(0, ti)
                        phase_a_tr(0, ti)

                for cg in range(8):
                    if cg == 2:
                        for fn in tail_items:
                            fn()
                        tail_items.clear()
                    ws = wload(win[cg])
                    w3 = ws.t[:].rearrange("p (k c) -> p k c", k=16)
                    bi = next_bank()
                    bh = next_bank()
                    for dk in range(16):
                        S.op("pe", lambda e, dk=dk: e.matmul(bank(bi), lhsT=w3[:, dk, :], rhs=hT.t[:, dk, 0:TBK],
                                                             start=(dk == 0), stop=(dk == 15)),
                             reads=[ws.buf, hT.buf], writes=[PB[bi]])
                    for dk in range(16):
                        S.op("pe", lambda e, dk=dk: e.matmul(bank(bh)[:, 0:16], lhsT=w3[:, dk, :],
                                                             rhs=hT.t[:, dk, TBK:EXT],
                                                             start=(dk == 0), stop=(dk == 15)),
                             reads=[ws.buf, hT.buf], writes=[PB[bh]])
                    S.op("act", lambda e: e.activation(out=aext.t[:, 16:EXT], in_=bank(bi), func=AF.Copy),
                         reads=[PB[bi]], writes=[aext.buf])
                    S.op("act", lambda e: e.activation(out=aext.t[:, 0:16], in_=bank(bh)[:, 0:16], func=AF.Copy),
                         reads=[PB[bh]], writes=[aext.buf])
                    g = cg // 2
                    w = 2 << g
                    cur = aext
                    off = 0
                    step = 1
                    tmps = [tmpA, tmpB]
                    ti2 = 0
                    while step < w:
                        nxt = tmps[ti2 % 2]
                        ti2 += 1
                        lo = off + step
                        S.op("dve", lambda e, cur=cur, nxt=nxt, lo=lo, step=step: e.tensor_tensor(
                            out=nxt.t[:, lo:EXT], in0=cur.t[:, lo:EXT], in1=cur.t[:, lo - step:EXT - step],
                            op=ALU.add), reads=[cur.buf], writes=[nxt.buf])
                        cur = nxt
                        off = lo
                        step *= 2
                    S.op("dve", lambda e, cur=cur, w=w, cg=cg: e.scalar_tensor_tensor(
                        out=pooledT.t[:, cg, :], in0=cur.t[:, 16:EXT], scalar=1.0 / w, in1=aext.t[:, 16:EXT],
                        op0=ALU.mult, op1=ALU.subtract), reads=[cur.buf, aext.buf], writes=[pooledT.buf])

                dump("pooledT", pooledT, pooledT.t[:], [128, 8, TBK], BF16)
                dump("hT", hT, hT.t[:], [128, 16, EXT], BF16)
                for g in range(4):
                    for oc in range(2):
                        bi = next_bank()
                        for cc in range(2):
                            S.op("pe", lambda e, cc=cc: e.matmul(bank(bi), lhsT=pw4[:, g * 2 + oc, cc, :],
                                                                 rhs=pooledT.t[:, 2 * g + cc, :],
                                                                 start=(cc == 0), stop=(cc == 1)),
                                 reads=[Spw.buf, pooledT.buf], writes=[PB[bi]])
                        S.op("act", lambda e: e.activation(out=ya1T.t[:, 2 * g + oc, :], in_=bank(bi),
                                                           func=AF.Copy,
                                                           scale=smt[:, 2 * g + oc:2 * g + oc + 1]),
                             reads=[PB[bi], Bcst], writes=[ya1T.buf])

                dump("ya1T", ya1T, ya1T.t[:], [128, 8, TBK], BF16)
                for ci in range(8):
                    wh = wload(win[8 + ci])
                    wb = wload(win[16 + ci])
                    wc = wload(win[24 + ci])
                    bh_m, bh_h, bc_m, bb_m = next_bank(), next_bank(), next_bank(), next_bank()
                    for (ws, bm, bhh) in ((wh, bh_m, bh_h), (wc, bc_m, bh_h), (wb, bb_m, None)):
                        w3 = ws.t[:].rearrange("p (k c) -> p k c", k=16)
                        for dk in range(16):
                            S.op("pe", lambda e, dk=dk, w3=w3, bm=bm: e.matmul(
                                bank(bm), lhsT=w3[:, dk, :], rhs=hT.t[:, dk, 0:TBK],
                                start=(dk == 0), stop=(dk == 15)),
                                reads=[ws.buf, hT.buf], writes=[PB[bm]])
                        if bhh is not None:
                            o = 0 if ws is wh else 16
                            for dk in range(16):
                                S.op("pe", lambda e, dk=dk, w3=w3, o=o: e.matmul(
                                    bank(bhh)[:, o:o + 16], lhsT=w3[:, dk, :], rhs=hT.t[:, dk, TBK:EXT],
                                    start=(dk == 0), stop=(dk == 15)),
                                    reads=[ws.buf, hT.buf], writes=[PB[bhh]])
                    S.op("act", lambda e: e.activation(out=tmpA.t[:, 16:EXT], in_=bank(bh_m), func=AF.Copy),
                         reads=[PB[bh_m]], writes=[tmpA.buf])
                    S.op("act", lambda e: e.activation(out=tmpA.t[:, 0:16], in_=bank(bh_h)[:, 0:16], func=AF.Copy),
                         reads=[PB[bh_h]], writes=[tmpA.buf])
                    S.op("dve", lambda e: e.tensor_tensor(out=aext.t[:, 16:EXT], in0=bank(bc_m),
                                                          in1=tmpA.t[:, 16:EXT], op=ALU.mult),
                         reads=[PB[bc_m], tmpA.buf], writes=[aext.buf])
                    S.op("dve", lambda e: e.tensor_tensor(out=aext.t[:, 0:16], in0=bank(bh_h)[:, 16:32],
                                                          in1=tmpA.t[:, 0:16], op=ALU.mult),
                         reads=[PB[bh_h], tmpA.buf], writes=[aext.buf])
                    cw = lambda kk: smt[:, 8 + ci * 3 + kk:8 + ci * 3 + kk + 1]
                    cb = smt[:, 32 + ci:33 + ci]
                    S.op("dve", lambda e: e.tensor_scalar(out=tmpB.t[:, 0:TBK], in0=aext.t[:, 16:EXT],
                                                          scalar1=cw(2), scalar2=cb, op0=ALU.mult, op1=ALU.add),
                         reads=[aext.buf, Bcst], writes=[tmpB.buf])
                    S.op("dve", lambda e: e.scalar_tensor_tensor(out=tmpA.t[:, 0:TBK], in0=aext.t[:, 15:EXT - 1],
                                                                 scalar=cw(1), in1=tmpB.t[:, 0:TBK],
                                                                 op0=ALU.mult, op1=ALU.add),
                         reads=[aext.buf, tmpB.buf, Bcst], writes=[tmpA.buf])
                    S.op("dve", lambda e: e.scalar_tensor_tensor(out=tmpB.t[:, 0:TBK], in0=aext.t[:, 14:EXT - 2],
                                                                 scalar=cw(0), in1=tmpA.t[:, 0:TBK],
                                                                 op0=ALU.mult, op1=ALU.add),
                         reads=[aext.buf, tmpA.buf, Bcst], writes=[tmpB.buf])
                    S.op("dve", lambda e: e.tensor_tensor(out=yb1T.t[:, ci, :], in0=bank(bb_m),
                                                          in1=tmpB.t[:, 0:TBK], op=ALU.mult),
                         reads=[PB[bb_m], tmpB.buf], writes=[yb1T.buf])

                dump("yb1T", yb1T, yb1T.t[:], [128, 8, TBK], BF16)
                for j in range(16):
                    wg0 = wload(win[32 + j])
                    wg1 = wload(win[48 + j])
                    wpc = wload(pcp[j])
                    b0, b1, ba, bb = next_bank(), next_bank(), next_bank(), next_bank()
                    for (ws, bi) in ((wg0, b0), (wg1, b1)):
                        w3 = ws.t[:].rearrange("p (k c) -> p k c", k=16)
                        for dk in range(16):
                            S.op("pe", lambda e, dk=dk, w3=w3, bi=bi: e.matmul(
                                bank(bi), lhsT=w3[:, dk, :], rhs=hT.t[:, dk, 0:TBK],
                                start=(dk == 0), stop=(dk == 15)),
                                reads=[ws.buf, hT.buf], writes=[PB[bi]])
                    w4 = wpc.t[:].rearrange("p (b k c) -> p b k c", b=2, k=8)
                    for (br, bi, src) in ((0, ba, ya1T), (1, bb, yb1T)):
                        for kc in range(8):
                            S.op("pe", lambda e, kc=kc, br=br, bi=bi, src=src: e.matmul(
                                bank(bi), lhsT=w4[:, br, kc, :], rhs=src.t[:, kc, :],
                                start=(kc == 0), stop=(kc == 7)),
                                reads=[wpc.buf, src.buf], writes=[PB[bi]])
                    S.op("act", lambda e: e.activation(out=sg0.t[:], in_=bank(b0), func=AF.Sigmoid),
                         reads=[PB[b0]], writes=[sg0.buf])
                    S.op("act", lambda e: e.activation(out=sg1.t[:], in_=bank(b1), func=AF.Sigmoid),
                         reads=[PB[b1]], writes=[sg1.buf])
                    S.op("dve", lambda e: e.tensor_tensor(out=sg0.t[:], in0=bank(ba), in1=sg0.t[:], op=ALU.mult),
                         reads=[PB[ba], sg0.buf], writes=[sg0.buf])
                    S.op("dve", lambda e: e.tensor_tensor(out=sg1.t[:], in0=bank(bb), in1=sg1.t[:], op=ALU.mult),
                         reads=[PB[bb], sg1.buf], writes=[sg1.buf])
                    S.op("dve", lambda e, j=j: e.tensor_tensor(out=mT.t[:, j, :], in0=sg0.t[:], in1=sg1.t[:],
                                                               op=ALU.add),
                         reads=[sg0.buf, sg1.buf], writes=[mT.buf])
                    if k + 1 < NBLK:
                        if j % 3 == 1:
                            phase_a_ln(k + 1, j // 3)
                        elif j % 3 == 0 and j > 0:
                            phase_a_tr(k + 1, j // 3 - 1)

                dump("mT", mT, mT.t[:], [128, 16, TBK], BF16)
                for pair in range(2):
                    for tp in range(2):
                        tl = pair * 2 + tp
                        gt = k * 4 + tl
                        T1 = LT[(tl % 2) * 2]
                        S.load("sp", T1, T1.t[:, :], h_s[gt * 128:(gt + 1) * 128, :], reads=[Bh_s[gt]])
                    for nch in range(4):
                        wo = WO[wo_i[0] % 2]
                        wo_i[0] += 1
                        S.load("pool", wo, wo.t[:].rearrange("p (a x) -> p a x", a=4),
                               wo_d[nch].rearrange("p (a x) -> p a x", a=4))
                        wo3 = wo.t[:].rearrange("p (k c) -> p k c", k=16)
                        for tp in range(2):
                            tl = pair * 2 + tp
                            bi = tp * 4 + nch
                            for dk in range(16):
                                S.op("pe", lambda e, dk=dk, tl=tl, bi=bi: e.matmul(
                                    bank(bi), lhsT=mT.t[:, dk, tl * 128:(tl + 1) * 128], rhs=wo3[:, dk, :],
                                    start=(dk == 0), stop=(dk == 15)),
                                    reads=[mT.buf, wo.buf], writes=[PB[bi]])
                    for tp in range(2):
                        tl = pair * 2 + tp
                        gt = k * 4 + tl
                        T1, T2, sms = LT[(tl % 2) * 2], LT[(tl % 2) * 2 + 1], LS[tl % 2]
                        pst = psA if tp == 0 else psB
                        S.op("dve", lambda e: e.scalar_tensor_tensor(
                            out=T1.t[:, :], in0=T1.t[:, :], scalar=ALPHA, in1=pst[:, :],
                            op0=ALU.mult, op1=ALU.add),
                            reads=[T1.buf] + PB[tp * 4:tp * 4 + 4], writes=[T1.buf])
                    for fn in pending_tr:
                        fn()
                    pending_tr.clear()
                    for tp in range(2):
                        tl = pair * 2 + tp
                        gt = k * 4 + tl
                        T1, T2, sms = LT[(tl % 2) * 2], LT[(tl % 2) * 2 + 1], LS[tl % 2]
                        layer_norm(128, T1, T2, sms, lnt[:, 2, :], lnt[:, 3, :], Bln)
                        S.store("sp", h1_s[gt * 128:(gt + 1) * 128, :], T2, T2.t[:, :], [Bh1_s[gt]])
                        pending_tr.append(lambda T2=T2, tl=tl, hT=hT: transpose_rows(128, T2, hT.t, tl * 128, hT.buf))
                tail_items.extend(pending_tr)
                pending_tr.clear()
                tail_items.append(lambda k=k, hT=hT: S.store(
                    "sp", h1T_s[k].rearrange("p (k t) -> p k t", k=16), hT, hT.t[:, :, 0:TBK], [Bh1T_s[k]]))
                if k == NBLK - 1:
                    for fn in tail_items:
                        fn()
                    tail_items.clear()

        if stage <= 1:
            S.barrier(engines=("sp",))
            return nc

        S.barrier()
        with ExitStack() as ph:
            ktt = sb(ph, "ktt", [128, 2048], BF16)
            Skt = Slot(S, "ktt", ktt)
            S.load("pool", Skt, ktt[:], kt_d[:, :])
            kt3 = ktt[:].rearrange("p (g n) -> p g n", g=16)
            H1 = [Slot(S, "h1b0", sb(ph, "h1b0", [128, 16, TBK], BF16))] * 2
            GST = [Slot(S, f"gst{i}", sb(ph, f"gst{i}", [128, TBK], BF16)) for i in range(3)]
            qT = Slot(S, "qT", sb(ph, "qT", [128, 16, TBK], BF16))
            NW = 6
            WR = [Slot(S, f"wq{i}", sb(ph, f"wq{i}", [128, 2048], BF16)) for i in range(NW)]
            Ssc = Slot(S, "Ssc", sb(ph, "Ssc", [128, 2048]))
            Ss2 = Slot(S, "Ss2", sb(ph, "Ss2", [128, 2048]))
            Vt = Slot(S, "Vt", sb(ph, "Vt", [128, 256]))
            It = Slot(S, "It", sb(ph, "It", [128, 256], U32))
            Ift = Slot(S, "Ift", sb(ph, "Ift", [128, 256]))
            Ct = Slot(S, "Ct", sb(ph, "Ct", [128, 2048]))
            C2t = Slot(S, "C2t", sb(ph, "C2t", [128, 2048]))
            Tt = Slot(S, "Tt", sb(ph, "Tt", [128, 128]))
            Jt = Slot(S, "Jt", sb(ph, "Jt", [128, 128], U32))
            thr = Slot(S, "thr", sb(ph, "thr", [128, 2048]))
            S.op("dve", lambda e: e.tensor_scalar(out=thr.t[:], in0=iota16, scalar1=16.0, scalar2=16.0,
                                                  op0=ALU.mult, op1=ALU.add), reads=[Bcst], writes=[thr.buf])
            Kf = Slot(S, "Kf", sb(ph, "Kf", [128, 2, 128]))
            Et = Slot(S, "Et", sb(ph, "Et", [128, 2048]))
            abg = Slot(S, "abg", sb(ph, "abg", [128, 3, 128]))
            gsm = Slot(S, "gsm", sb(ph, "gsm", [128, 16]))
            ABG = [Slot(S, f"ABG{i}", sb(ph, f"ABG{i}", [128, 4, 128])) for i in range(2)]
            DAB = [Slot(S, f"dab{i}", sb(ph, f"dab{i}", [128, 128], BF16)) for i in range(4)]
            NOH = 12
            OA = [Slot(S, f"oa{i}", sb(ph, f"oa{i}", [128, 128], BF16)) for i in range(NOH)]
            OB = [Slot(S, f"ob{i}", sb(ph, f"ob{i}", [128, 128], BF16)) for i in range(NOH)]
            Gb = [Slot(S, f"Gb{i}", sb(ph, f"Gb{i}", [128, 128, 128], BF16)) for i in range(2)]
            iob = Slot(S, "iob", sb(ph, "iob", [128, 128], BF16))
            S.op("act", lambda e: e.activation(out=iob.t[:], in_=iota128, func=AF.Copy),
                 reads=[Bcst], writes=[iob.buf])
            wi = 0
            oh_i = 0
            VG = [Buf(f"VG{i}") for i in range(16)]
            IG = [Buf(f"IG{i}") for i in range(16)]
            S2G = [Buf(f"S2G{i}") for i in range(16)]
            TG = [Buf(f"TG{i}") for i in range(8)]
            JG = [Buf(f"JG{i}") for i in range(8)]
            C2G = [Buf(f"C2G{i}") for i in range(8)]
            def q_block(k):
                nonlocal wi
                hb = H1[k % 2]
                S.load("sp", hb, hb.t[:].rearrange("p k t -> p (k t)"), h1T_s[k], reads=[Bh1T_s[k]])
                for cg in range(16):
                    ws = WR[wi % NW]
                    wi += 1
                    S.load("pool", ws, ws.t[:], wq_d[cg])
                    w3 = ws.t[:].rearrange("p (k c) -> p k c", k=16)
                    bi = wi % 4
                    for dk in range(16):
                        S.op("pe", lambda e, dk=dk: e.matmul(bank(bi), lhsT=w3[:, dk, :], rhs=hb.t[:, dk, :],
                                                             start=(dk == 0), stop=(dk == 15)),
                             reads=[ws.buf, hb.buf], writes=[PB[bi]])
                    S.op("act", lambda e, cg=cg: e.activation(out=qT.t[:, cg, :], in_=bank(bi), func=AF.Copy),
                         reads=[PB[bi]], writes=[qT.buf])
                    yield

            def topk_gen(k, tl):
                hb = H1[k % 2]
                if tl == 0:
                    yield from q_block(k)
                gt = k * 4 + tl
                for g in range(16):
                    S.op("pe", lambda e, g=g: e.matmul(psA[:, g * 128:(g + 1) * 128],
                                                       lhsT=qT.t[:, g, tl * 128:(tl + 1) * 128],
                                                       rhs=kt3[:, g, :], start=True, stop=True),
                         reads=[qT.buf, Skt.buf], writes=[PB[g // 4]])
                S.op("act", lambda e: e.activation(out=Ssc.t[:], in_=psA[:, :], func=AF.Copy),
                     reads=PB[0:4], writes=[Ssc.buf])
                s3 = Ssc.t[:].rearrange("p (g n) -> p g n", g=16)
                s23 = Ss2.t[:].rearrange("p (g n) -> p g n", g=16)
                v3 = Vt.t[:].rearrange("p (g k) -> p g k", g=16)
                i3 = It.t[:].rearrange("p (g k) -> p g k", g=16)
                for ph_ in range(5):
                    for g in range(16):
                        if ph_ == 0:
                            S.op("dve", lambda e, g=g: e.max(out=v3[:, g, 0:8], in_=s3[:, g, :]),
                                 reads=[Ssc.buf], writes=[VG[g]])
                        elif ph_ == 1:
                            S.op("dve", lambda e, g=g: e.max_index(out=i3[:, g, 0:8], in_max=v3[:, g, 0:8],
                                                                   in_values=s3[:, g, :]),
                                 reads=[Ssc.buf, VG[g]], writes=[IG[g]])
                        elif ph_ == 2:
                            S.op("dve", lambda e, g=g: e.match_replace(out=s23[:, g, :],
                                                                       in_to_replace=v3[:, g, 0:8],
                                                                       in_values=s3[:, g, :], imm_value=NEG),
                                 reads=[Ssc.buf, VG[g]], writes=[S2G[g]])
                        elif ph_ == 3:
                            S.op("dve", lambda e, g=g: e.max(out=v3[:, g, 8:16], in_=s23[:, g, :]),
                                 reads=[S2G[g]], writes=[VG[g]])
                        else:
                            S.op("dve", lambda e, g=g: e.max_index(out=i3[:, g, 8:16], in_max=v3[:, g, 8:16],
                                                                   in_values=s23[:, g, :]),
                                 reads=[S2G[g], VG[g]], writes=[IG[g]])
                        if g % 8 == 7:
                            yield
                S.op("dve", lambda e: e.tensor_copy(out=Ift.t[:], in_=It.t[:]),
                     reads=IG, writes=[Ift.buf])
                v4 = Vt.t[:].rearrange("p (h s k) -> p h s k", h=8, s=2)
                if4 = Ift.t[:].rearrange("p (h s k) -> p h s k", h=8, s=2)
                c4 = Ct.t[:].rearrange("p (h a b) -> p h a b", h=8, a=16)
                c24 = C2t.t[:].rearrange("p (h a b) -> p h a b", h=8, a=16)
                S.op("dve", lambda e: e.tensor_tensor(
                    out=c4, in0=v4[:, :, 0, :].unsqueeze(3).broadcast_to([128, 8, 16, 16]),
                    in1=v4[:, :, 1, :].unsqueeze(2).broadcast_to([128, 8, 16, 16]), op=ALU.add),
                    reads=VG, writes=[Ct.buf])
                c3 = Ct.t[:].rearrange("p (h x) -> p h x", h=8)
                c23 = C2t.t[:].rearrange("p (h x) -> p h x", h=8)
                t3 = Tt.t[:].rearrange("p (h k) -> p h k", h=8)
                j3 = Jt.t[:].rearrange("p (h k) -> p h k", h=8)
                for ph_ in range(5):
                    for h in range(8):
                        if ph_ == 0:
                            S.op("dve", lambda e, h=h: e.max(out=t3[:, h, 0:8], in_=c3[:, h, :]),
                                 reads=[Ct.buf], writes=[TG[h]])
                        elif ph_ == 1:
                            S.op("dve", lambda e, h=h: e.max_index(out=j3[:, h, 0:8], in_max=t3[:, h, 0:8],
                                                                   in_values=c3[:, h, :]),
                                 reads=[Ct.buf, TG[h]], writes=[JG[h]])
                        elif ph_ == 2:
                            S.op("dve", lambda e, h=h: e.match_replace(out=c23[:, h, :],
                                                                       in_to_replace=t3[:, h, 0:8],
                                                                       in_values=c3[:, h, :], imm_value=NEG),
                                 reads=[Ct.buf, TG[h]], writes=[C2G[h]])
                        elif ph_ == 3:
                            S.op("dve", lambda e, h=h: e.max(out=t3[:, h, 8:16], in_=c23[:, h, :]),
                                 reads=[C2G[h]], writes=[TG[h]])
                        else:
                            S.op("dve", lambda e, h=h: e.max_index(out=j3[:, h, 8:16], in_max=t3[:, h, 8:16],
                                                                   in_values=c23[:, h, :]),
                                 reads=[C2G[h], TG[h]], writes=[JG[h]])
                    yield
                S.op("dve", lambda e: e.tensor_copy(out=Kf.t[:, 1, :], in_=Jt.t[:]),
                     reads=JG, writes=[Kf.buf])
                S.op("dve", lambda e: e.tensor_tensor(
                    out=Et.t[:].rearrange("p (x j) -> p x j", j=16),
                    in0=Kf.t[:, 1, :].unsqueeze(2).broadcast_to([128, 128, 16]),
                    in1=thr.t[:].rearrange("p (x j) -> p x j", j=16), op=ALU.is_ge),
                    reads=[Kf.buf, thr.buf], writes=[Et.buf])
                S.op("dve", lambda e: e.tensor_reduce(
                    out=Kf.t[:, 0, :], in_=Et.t[:].rearrange("p (x j) -> p x j", j=16),
                    axis=AX.X, op=ALU.add), reads=[Et.buf], writes=[Kf.buf])
                S.op("dve", lambda e: e.scalar_tensor_tensor(
                    out=Kf.t[:, 1, :], in0=Kf.t[:, 0, :], scalar=-16.0, in1=Kf.t[:, 1, :],
                    op0=ALU.mult, op1=ALU.add), reads=[Kf.buf], writes=[Kf.buf])
                e4 = Et.t[:].rearrange("p (h k j) -> p h k j", h=8, k=16)
                io4 = iota16.rearrange("p (h k j) -> p h k j", h=8, k=16)
                for s in range(2):
                    kf3 = Kf.t[:, s, :].rearrange("p (h k) -> p h k", h=8)
                    S.op("dve", lambda e, kf3=kf3: e.tensor_tensor(
                        out=e4, in0=kf3.unsqueeze(3).broadcast_to([128, 8, 16, 16]), in1=io4,
                        op=ALU.is_equal), reads=[Kf.buf, Bcst], writes=[Et.buf])
                    S.op("dve", lambda e, s=s: e.tensor_tensor(
                        out=e4, in0=e4, in1=if4[:, :, s, :].unsqueeze(2).broadcast_to([128, 8, 16, 16]),
                        op=ALU.mult), reads=[Et.buf, Ift.buf], writes=[Et.buf])
                    S.op("dve", lambda e, s=s: e.tensor_reduce(
                        out=abg.t[:, s, :], in_=Et.t[:].rearrange("p (x j) -> p x j", j=16),
                        axis=AX.X, op=ALU.add), reads=[Et.buf], writes=[abg.buf])
                g3 = abg.t[:, 2, :].rearrange("p (h k) -> p h k", h=8)
                S.op("dve", lambda e: e.tensor_tensor(
                    out=g3, in0=t3, in1=t3[:, :, 0:1].broadcast_to([128, 8, 16]), op=ALU.subtract),
                    reads=TG, writes=[abg.buf])
                S.op("act", lambda e: e.activation(out=abg.t[:, 2, :], in_=abg.t[:, 2, :], func=AF.Exp),
                     reads=[abg.buf], writes=[abg.buf])
                S.op("dve", lambda e: e.tensor_reduce(out=gsm.t[:, 0:8], in_=g3, axis=AX.X, op=ALU.add),
                     reads=[abg.buf], writes=[gsm.buf])
                S.op("dve", lambda e: e.reciprocal(out=gsm.t[:, 8:16], in_=gsm.t[:, 0:8]),
                     reads=[gsm.buf], writes=[gsm.buf])
                S.op("dve", lambda e: e.tensor_tensor(
                    out=g3, in0=g3, in1=gsm.t[:, 8:16].unsqueeze(2).broadcast_to([128, 8, 16]),
                    op=ALU.mult), reads=[abg.buf, gsm.buf], writes=[abg.buf])
                AB = ABG[gt % 2]
                bi = 0
                for s in range(3):
                    S.op("pe", lambda e, s=s: e.transpose(out=bank(bi)[:, s * 128:(s + 1) * 128],
                                                          in_=abg.t[:, s, :], identity=ident),
                         reads=[abg.buf, Bcst], writes=[PB[bi]])
                S.op("act", lambda e: e.activation(out=AB.t[:, 0:3, :].rearrange("p s t -> p (s t)"),
                                                   in_=bank(bi)[:, 0:384], func=AF.Copy),
                     reads=[PB[bi]], writes=[AB.buf])
                S.op("dve", lambda e: e.tensor_scalar(out=AB.t[:, 3, :], in0=AB.t[:, 1, :], scalar1=-1.0,
                                                      scalar2=None, op0=ALU.mult),
                     reads=[AB.buf], writes=[AB.buf])

            def onehot_gen(gt):
                nonlocal oh_i
                AB = ABG[gt % 2]
                G = Gb[gt % 2]
                pend_ev = None
                pend_mm = None
                for t4 in range(32):
                    bi = 6 + (t4 % 2)
                    grp_ = []
                    for tt in range(4):
                        t = t4 * 4 + tt
                        grp_.append((tt, t, OA[oh_i % NOH], OB[oh_i % NOH], DAB[oh_i % 4]))
                        oh_i += 1
                    for (tt, t, oa, ob, da) in grp_:
                        S.op("dve", lambda e, t=t, oa=oa: e.tensor_scalar(
                            out=oa.t[:], in0=iob.t[:], scalar1=AB.t[:, 0, t:t + 1], scalar2=AB.t[:, 2, t:t + 1],
                            op0=ALU.is_equal, op1=ALU.mult), reads=[AB.buf, iob.buf], writes=[oa.buf])
                    for (tt, t, oa, ob, da) in grp_:
                        if tt == 0:
                            S.op("act", lambda e, t=t, da=da: e.activation(
                                out=da.t[:], in_=iob.t[:], func=AF.Abs, bias=AB.t[:, 3, t:t + 1], scale=1.0),
                                reads=[AB.buf, iob.buf], writes=[da.buf])
                        else:
                            S.op("dve", lambda e, t=t, ob=ob: e.tensor_scalar(
                                out=ob.t[:], in0=iob.t[:], scalar1=AB.t[:, 1, t:t + 1], scalar2=None,
                                op0=ALU.is_equal), reads=[AB.buf, iob.buf], writes=[ob.buf])
                    for (tt, t, oa, ob, da) in grp_:
                        if tt == 0:
                            S.op("act", lambda e, ob=ob, da=da: e.activation(
                                out=ob.t[:], in_=da.t[:], func=AF.Relu, bias=1.0, scale=-1.0),
                                reads=[da.buf], writes=[ob.buf])
                    def mm_ev(t4=t4, bi=bi, G=G, grp_=grp_):
                        for (tt, t, oa, ob, da) in grp_:
                            S.op("pe", lambda e, tt=tt, oa=oa, ob=ob: e.matmul(
                                bank(bi).rearrange("p (i t) -> p t i", t=4)[:, tt, :], lhsT=ob.t[:], rhs=oa.t[:],
                                start=True, stop=True), reads=[oa.buf, ob.buf], writes=[PB[bi]])

                        def ev():
                            S.op("act", lambda e: e.activation(
                                out=G.t[:, :, t4 * 4:(t4 + 1) * 4],
                                in_=bank(bi).rearrange("p (i t) -> p i t", t=4), func=AF.Copy),
                                reads=[PB[bi]], writes=[G.buf])
                        return ev
                    if pend_mm is not None:
                        new_ev = pend_mm()
                        if pend_ev is not None:
                            pend_ev()
                        pend_ev = new_ev
                    pend_mm = mm_ev
                    yield
                new_ev = pend_mm()
                if pend_ev is not None:
                    pend_ev()
                new_ev()
                S.store("sp", G_s[gt], G, G.t[:].rearrange("p i t -> p (i t)"), [BG_s[gt]])


            def drive2(ga, gb):
                da = ga is None
                db = gb is None
                while not (da and db):
                    if not da:
                        try:
                            next(ga)
                        except StopIteration:
                            da = True
                    if not db:
                        try:
                            next(gb)
                        except StopIteration:
                            db = True

            early_chunks = [c for c in range(128) if is_early(c)]

            def early_gen(k, chunks):
                nonlocal wi
                hb = H1[0]
                for c in chunks:
                    ws = WR[wi % NW]
                    wi += 1
                    S.load("pool", ws, ws.t[:], ut_d[c])
                    u3 = ws.t[:].rearrange("p (k e) -> p k e", k=16)
                    bi = 4 + (c % 2)
                    for dk in range(16):
                        S.op("pe", lambda e, dk=dk: e.matmul(bank(bi), lhsT=u3[:, dk, :], rhs=hb.t[:, dk, :],
                                                             start=(dk == 0), stop=(dk == 15)),
                             reads=[ws.buf, hb.buf], writes=[PB[bi]])
                        if dk % 4 == 3:
                            yield
                    st = GST[eidx(c) % 3]
                    S.op("act", lambda e: e.activation(out=st.t[:], in_=bank(bi), func=AF.Gelu),
                         reads=[PB[bi]], writes=[st.buf])
                    S.store("sp", GE_s[eidx(c), :, k * TBK:(k + 1) * TBK], st, st.t[:], [BGE[eidx(c)][k]])

            def drive3(ga, gb, gc, nc_per_round):
                da = ga is None
                db = gb is None
                dcx = gc is None
                while not (da and db):
                    if not da:
                        try:
                            next(ga)
                        except StopIteration:
                            da = True
                    if not db:
                        try:
                            next(gb)
                        except StopIteration:
                            db = True
                    if not dcx:
                        for _ in range(nc_per_round):
                            try:
                                next(gc)
                            except StopIteration:
                                dcx = True
                                break
                if not dcx:
                    for _ in gc:
                        pass

            tiles = [(k, tl) for k in range(NBLK) for tl in range(4)]
            n4 = (len(early_chunks) + 3) // 4
            eparts = [early_chunks[q * n4:(q + 1) * n4] for q in range(4)]
            drive3(topk_gen(0, 0), None, early_gen(0, eparts[0]), 4)
            for i in range(16):
                k, tl = tiles[i]
                nxt = topk_gen(*tiles[i + 1]) if i + 1 < 16 else None
                eg = None
                if tl < 3:
                    eg = early_gen(k, eparts[tl + 1])
                elif k + 1 < NBLK:
                    eg = early_gen(k + 1, eparts[0])
                drive3(onehot_gen(i), nxt, eg, 3)

        if stage <= 2:
            S.barrier(engines=("sp",))
            return nc

        S.barrier()
        with ExitStack() as ph:
            lnt = sb(ph, "lnt2", [128, 2, D])
            Bln = Buf("lnt2")
            for i in range(2):
                S.dma("sp", lnt[:, i, :], lnv[4 + i], dc2, writes=[Bln])
            Bln.w = {dc2.sem: dc2.cnt}
            LS = [Slot(S, f"lv{i}", sb(ph, f"lv{i}", [128, 32])) for i in range(1)] * 2
            h1h = Slot(S, "h1h", sb(ph, "h1h", [128, 2, 16, TBK], BF16))
            acc = [Slot(S, f"acc{i}", sb(ph, f"acc{i}", [128, D])) for i in range(8)]
            CG = 4
            Gg = [Slot(S, f"Gg{i}", sb(ph, f"Gg{i}", [128, 8, CG * 128], BF16)) for i in range(2)]
            NU = 2
            UR = [Slot(S, f"ur{i}", sb(ph, f"ur{i}", [128, 2048], BF16)) for i in range(NU)]
            NV = 8
            vr_all = sb(ph, "vr_all", [128, NV * 2048], BF16)
            VR = [Slot(S, f"vr{i}", vr_all[:, i * 2048:(i + 1) * 2048]) for i in range(NV)]
            LT = [Slot(S, f"lu{i}", vr_all[:, i * 4096:(i + 1) * 4096].bitcast(F32)) for i in range(2)] * 2
            GE = [Slot(S, f"ge{i}", sb(ph, f"ge{i}", [128, 1024], BF16)) for i in range(2)]
            GE4 = [Slot(S, f"gq{i}", sb(ph, f"gq{i}", [128, 1024], BF16)) for i in range(4)]
            CF = [Slot(S, f"cf{i}", sb(ph, f"cf{i}", [128, CG, 1024], BF16)) for i in range(2)]
            ui = 0
            vi = 0
            gi = 0
            for hf in range(2):
                for b2 in range(2):
                    S.load("sp", h1h, h1h.t[:, b2].rearrange("p k t -> p (k t)"), h1T_s[2 * hf + b2],
                           reads=[Bh1T_s[2 * hf + b2]])
                ngrp = 128 // CG
                issued = [0]

                def ensure_loaded(upto):
                    while issued[0] <= min(upto, NE - 1):
                        ei = issued[0]
                        slot = GE4[ei % 4]
                        S.load("sp", slot, slot.t[:], GE_s[ei, :, hf * 1024:(hf + 1) * 1024],
                               reads=[BGE[ei][2 * hf], BGE[ei][2 * hf + 1]])
                        issued[0] += 1

                ensure_loaded(3)

                def load_G(g):
                    Gs_ = Gg[g % 2]
                    S.load("sp", Gs_, Gs_.t[:],
                           G_s[8 * hf:8 * hf + 8, :, g * CG * 128:(g + 1) * CG * 128].rearrange("t p x -> p t x"),
                           reads=BG_s[8 * hf:8 * hf + 8])

                load_G(0)

                def e1(grp):
                    nonlocal ui
                    Gs = Gg[grp % 2]
                    cf = CF[grp % 2]
                    for c in range(CG):
                        if c == 2 and grp + 1 < ngrp:
                            load_G(grp + 1)
                        cgl = grp * CG + c
                        ge = GE[cgl % 2]
                        if is_early(cgl):
                            ensure_loaded(eidx(cgl) + 3)
                            ge = GE4[eidx(cgl) % 4]
                            S.op("dve", lambda e, c=c: e.tensor_tensor(
                                out=cf.t[:, c, :].rearrange("p (t x) -> p t x", t=8),
                                in0=ge.t[:].rearrange("p (t x) -> p t x", t=8),
                                in1=Gs.t[:, :, c * 128:(c + 1) * 128], op=ALU.mult),
                                reads=[ge.buf, Gs.buf], writes=[cf.buf])
                            yield
                            continue
                        us = UR[ui % NU]
                        ui += 1
                        S.load("pool", us, us.t[:], ut_d[cgl])
                        u3 = us.t[:].rearrange("p (k e) -> p k e", k=16)
                        pb = (cgl % 2) * 2
                        for tc in range(2):
                            for dk in range(16):
                                S.op("pe", lambda e, tc=tc, dk=dk: e.matmul(
                                    bank(pb + tc), lhsT=u3[:, dk, :], rhs=h1h.t[:, tc, dk, :],
                                    start=(dk == 0), stop=(dk == 15)),
                                    reads=[us.buf, h1h.buf], writes=[PB[pb + tc]])
                                if dk % 4 == 3:
                                    yield
                        ge = GE[cgl % 2]
                        S.op("act", lambda e: e.activation(out=ge.t[:], in_=psA[:, pb * 512:(pb + 2) * 512],
                                                           func=AF.Gelu),
                             reads=[PB[pb], PB[pb + 1]], writes=[ge.buf])
                        S.op("dve", lambda e, c=c: e.tensor_tensor(
                            out=cf.t[:, c, :].rearrange("p (t x) -> p t x", t=8),
                            in0=ge.t[:].rearrange("p (t x) -> p t x", t=8),
                            in1=Gs.t[:, :, c * 128:(c + 1) * 128], op=ALU.mult),
                            reads=[ge.buf, Gs.buf], writes=[cf.buf])

                def e2(grp):
                    nonlocal vi
                    cf = CF[grp % 2]
                    vs = []
                    for c in range(CG):
                        v = VR[vi % NV]
                        vi += 1
                        S.load("pool", v, v.t[:], vv_d[grp * CG + c])
                        vs.append(v)
                    for tl in range(8):
                        for dh in range(2):
                            pb = 4 + ((tl * 2 + dh) % 2) * 2
                            for c in range(CG):
                                for n in range(2):
                                    S.op("pe", lambda e, c=c, n=n: e.matmul(
                                        bank(pb + n), lhsT=cf.t[:, c, tl * 128:(tl + 1) * 128],
                                        rhs=vs[c].t[:, dh * 1024 + n * 512:dh * 1024 + (n + 1) * 512],
                                        start=(c == 0), stop=(c == CG - 1)),
                                        reads=[cf.buf, vs[c].buf], writes=[PB[pb + n]])
                            src = psB[:, (pb - 4) * 512:(pb - 4 + 2) * 512]
                            dst = acc[tl].t[:, dh * 1024:(dh + 1) * 1024]
                            if grp == 0:
                                S.op("act", lambda e: e.activation(out=dst, in_=src, func=AF.Copy),
                                     reads=[PB[pb], PB[pb + 1]], writes=[acc[tl].buf])
                            else:
                                S.op("dve", lambda e: e.tensor_tensor(out=dst, in0=src, in1=dst, op=ALU.add),
                                     reads=[PB[pb], PB[pb + 1], acc[tl].buf], writes=[acc[tl].buf])
                            yield

                def drive(ga, gb):
                    da = ga is None
                    db = gb is None
                    while not (da and db):
                        if not db:
                            try:
                                next(gb)
                            except StopIteration:
                                db = True
                        if not da:
                            try:
                                next(ga)
                            except StopIteration:
                                da = True

                drive(e1(0), None)
                for grp in range(ngrp):
                    drive(e1(grp + 1) if grp + 1 < ngrp else None, e2(grp))
                S.barrier()
                for tl in range(8):
                    gt = hf * 8 + tl
                    T1, T2, sms = LT[(tl % 2) * 2], LT[(tl % 2) * 2 + 1], LS[tl % 2]
                    S.load("sp", T2, T2.t[:, :], h1_s[gt * 128:(gt + 1) * 128, :], reads=[Bh1_s[gt]])
                    S.op("dve", lambda e: e.scalar_tensor_tensor(
                        out=T1.t[:, :], in0=T2.t[:, :], scalar=ALPHA, in1=acc[tl].t[:, :],
                        op0=ALU.mult, op1=ALU.add), reads=[T2.buf, acc[tl].buf], writes=[T1.buf])
                    layer_norm(128, T1, T2, sms, lnt[:, 0, :], lnt[:, 1, :], Bln)
                    S.store("sp", out_d[gt * 128:(gt + 1) * 128, :], T2, T2.t[:, :], [Bout[gt]])
                S.barrier()
        S.barrier(engines=("sp",))
    return nc


def prep_shared(inp):
    f = np.float32
    L = 0
    sh = {}
    vecs = [inp["ln_in_g"], inp["ln_in_b"], inp["ln1_g"][L], inp["ln1_b"][L], inp["ln2_g"][L], inp["ln2_b"][L]]
    sh["lnv"] = np.ascontiguousarray(np.broadcast_to(np.stack(vecs)[:, None, :], (6, 128, D))).astype(f)
    sm = np.zeros((128, 40), f)
    sm[:, 0:8] = inp["pool_scale"][L].reshape(8, 128).T
    cw = inp["conv_w"][L][:, 0, :]
    sm[:, 8:32] = cw.reshape(3, 8, 128).transpose(2, 1, 0).reshape(128, 24)
    sm[:, 32:40] = inp["conv_b"][L].reshape(8, 128).T
    sh["sm"] = sm
    cst = np.zeros((128, 128 + 128 + 2048), f)
    cst[:, 0:128] = np.eye(128, dtype=f)
    cst[:, 128:256] = np.arange(128, dtype=f)[None, :]
    cst[:, 256:] = (np.arange(2048) % 16).astype(f)[None, :]
    sh["cst"] = cst

    def colgroups(w, kc):
        n = w.shape[1] // 128
        return np.ascontiguousarray(w.reshape(kc, 128, n, 128).transpose(2, 1, 0, 3).reshape(n, 128, kc * 128))
    sh["win"] = colgroups(inp["w_in"][L], 16)
    pwm = inp["pool_w"][L]
    sh["pw"] = np.ascontiguousarray(
        pwm.reshape(4, 2, 128, 2, 128).transpose(2, 0, 3, 1, 4).reshape(128, 2048))
    pp = colgroups(inp["pool_proj"][L], 8)
    cp = colgroups(inp["conv_proj"][L], 8)
    sh["pcp"] = np.ascontiguousarray(np.concatenate([pp, cp], axis=2))
    wo = inp["w_out"][L]
    sh["wo"] = np.ascontiguousarray(wo.reshape(16, 128, 4, 512).transpose(2, 1, 0, 3).reshape(4, 128, 8192))
    sh["wq"] = colgroups(inp["peer_wq"][L], 16)
    k1 = inp["peer_keys_1"][L]
    k2 = inp["peer_keys_2"][L]
    kt = np.stack([k1, k2], axis=1)
    sh["kt"] = np.ascontiguousarray(kt.transpose(3, 0, 1, 2).reshape(128, 2048))
    U = inp["expert_u"][L]
    sh["ut"] = np.ascontiguousarray(U.reshape(128, 128, 16, 128).transpose(0, 3, 2, 1).reshape(128, 128, 2048))
    sh["vv"] = np.ascontiguousarray(inp["expert_v"][L].reshape(128, 128, 2048))
    return sh


def prep_x(inp):
    x = inp["x"]
    meta = inp["meta_tokens"]
    xs = []
    for c in range(NCORE):
        b, j = divmod(c, 4)
        full = np.concatenate([meta, x[b]], axis=0)
        blk = np.empty((NBLK, EXT, D), np.float32)
        for k in range(NBLK):
            p0 = 16 + 2048 * j + TBK * k
            blk[k, 0:TBK] = full[p0:p0 + TBK]
            blk[k, TBK:EXT] = full[p0 - 16:p0]
        xs.append(blk)
    return xs


_NC_CACHE = {}


def kernel(**inputs):
    inp = {k: np.asarray(v) for k, v in inputs.items()}
    sh = prep_shared(inp)
    xs = prep_x(inp)
    if "nc" not in _NC_CACHE:
        _NC_CACHE["nc"] = build_program()
    nc = _NC_CACHE["nc"]
    in_maps = []
    for c in range(NCORE):
        m = dict(sh)
        m["x_in"] = xs[c]
        in_maps.append(m)
    res = run_bass_kernel_spmd(nc, in_maps, core_ids=list(range(NCORE)))
    out = np.empty((2, 8192, D), np.float32)
    for c in range(NCORE):
        b, j = divmod(c, 4)
        out[b, 2048 * j:2048 * (j + 1)] = np.asarray(res.results[c]["out"])
    return out
```

```python
import numpy as np
import concourse.bass as bass
import concourse.mybir as mybir
from concourse.bass_utils import run_bass_kernel_spmd
from contextlib import ExitStack

F32 = mybir.dt.float32
BF16 = mybir.dt.bfloat16
U32 = mybir.dt.uint32
ALU = mybir.AluOpType
AF = mybir.ActivationFunctionType
AX = mybir.AxisListType

D = 2048
NCORE = 8
TOK = 2048
TBK = 512
NBLK = 4
EXT = TBK + 16
ALPHA = 2.0 ** 0.25
EPS = 1e-5
NEG = -1.0e30


class Buf:
    __slots__ = ("name", "w", "r")

    def __init__(self, name=""):
        self.name = name
        self.w = {}
        self.r = {}


class DSem:
    __slots__ = ("sem", "cnt")

    def __init__(self, sem):
        self.sem = sem
        self.cnt = 0


class Slot:
    def __init__(self, S, name, t):
        self.t = t
        self.buf = Buf(name)
        self.name = name
        self.S = S
        self._din = None
        self._dout = None

    @property
    def din(self):
        if self._din is None:
            self._din = self.S.dsem(self.name + "_i")
        return self._din

    @property
    def dout(self):
        if self._dout is None:
            self._dout = self.S.dsem(self.name + "_o")
        return self._dout


class Sched:
    EPOCH = 20000

    def __init__(self, nc, es):
        self.nc = nc
        self.es = es
        self.eng = {"pe": nc.tensor, "act": nc.scalar, "dve": nc.vector,
                    "pool": nc.gpsimd, "sp": nc.sync}
        self.sem = {}
        self.cnt = {}
        self.allsem = []
        self.dsems = []
        self.waited = {k: {} for k in self.eng}
        self.nsem = 0
        for k in self.eng:
            self._new_epoch(k)

    def _new_epoch(self, k):
        self.nsem += 1
        self.sem[k] = self.es.enter_context(self.nc.semaphore(f"e_{k}_{self.nsem}"))
        self.cnt[k] = 0
        self.allsem.append([self.sem[k], 0])

    def dsem(self, name):
        self.nsem += 1
        d = DSem(self.es.enter_context(self.nc.semaphore("d_" + name)))
        self.dsems.append(d)
        return d

    def _wait(self, k, deps):
        E = self.eng[k]
        wd = self.waited[k]
        for sem, val in deps.items():
            if wd.get(sem, 0) >= val:
                continue
            E.wait_ge(sem, val)
            wd[sem] = val

    def _deps(self, k, reads, writes):
        deps = {}
        for b in reads:
            for s, v in b.w.items():
                if deps.get(s, 0) < v:
                    deps[s] = v
        for b in writes:
            for s, v in b.w.items():
                if deps.get(s, 0) < v:
                    deps[s] = v
            for s, v in b.r.items():
                if deps.get(s, 0) < v:
                    deps[s] = v
        if k == "pe":
            deps.pop(self.sem["pe"], None)
        return deps

    def op(self, k, fn, reads=(), writes=()):
        self._wait(k, self._deps(k, reads, writes))
        ins = fn(self.eng[k])
        if self.cnt[k] >= self.EPOCH:
            self._new_epoch(k)
        self.cnt[k] += 1
        ins.then_inc(self.sem[k], 1)
        s, v = self.sem[k], self.cnt[k]
        for e in self.allsem:
            if e[0] is s:
                e[1] = v
        for b in reads:
            if b.r.get(s, 0) < v:
                b.r[s] = v
        for b in writes:
            b.w = {s: v}
            b.r = {}
        return ins

    def dma(self, q, out, in_, ds, reads=(), writes=(), **kw):
        self._wait(q, self._deps(q, reads, writes))
        ins = self.eng[q].dma_start(out=out, in_=in_, **kw)
        ins.then_inc(ds.sem, 16)
        ds.cnt += 16
        for b in reads:
            if b.r.get(ds.sem, 0) < ds.cnt:
                b.r[ds.sem] = ds.cnt
        for b in writes:
            b.w = {ds.sem: ds.cnt}
            b.r = {}
        return ins

    def load(self, q, slot, dst_ap, src_ap, reads=(), **kw):
        return self.dma(q, dst_ap, src_ap, slot.din, reads=reads, writes=[slot.buf], **kw)

    def store(self, q, dst_ap, slot, src_ap, dram_bufs=(), **kw):
        return self.dma(q, dst_ap, src_ap, slot.dout, reads=[slot.buf], writes=list(dram_bufs), **kw)

    def barrier(self, engines=("pe", "act", "dve", "pool", "sp")):
        deps = {}
        for s, v in self.allsem:
            if v > 0:
                deps[s] = v
        for d in self.dsems:
            if d.cnt > 0:
                deps[d.sem] = d.cnt
        for k in engines:
            dd = dict(deps)
            if k == "pe":
                dd.pop(self.sem["pe"], None)
            self._wait(k, dd)


def build_program(stage=99, dbg=False):
    nc = bass.Bass("TRN2", target_bir_lowering=False)

    def din(name, shape, dt=F32):
        return nc.dram_tensor(name, shape, dt, kind="ExternalInput").ap()

    def dscr(name, shape, dt=F32):
        return nc.dram_tensor(name, shape, dt, kind=("ExternalOutput" if dbg else "Internal")).ap()

    x_in = din("x_in", [NBLK, EXT, D])
    lnv = din("lnv", [6, 128, D])
    sm_d = din("sm", [128, 40])
    cst_d = din("cst", [128, 128 + 128 + 2048])
    win = din("win", [64, 128, 2048])
    pw_d = din("pw", [128, 2048])
    pcp = din("pcp", [16, 128, 2048])
    wo_d = din("wo", [4, 128, 8192])
    wq_d = din("wq", [16, 128, 2048])
    kt_d = din("kt", [128, 2048])
    ut_d = din("ut", [128, 128, 2048])
    vv_d = din("vv", [128, 128, 2048])
    out_d = nc.dram_tensor("out", [TOK, D], F32, kind="ExternalOutput").ap()

    h_s = dscr("h_s", [TOK, D])
    h1_s = dscr("h1_s", [TOK, D])
    h1T_s = dscr("h1T_s", [NBLK, 128, 16 * TBK], BF16)
    G_s = dscr("G_s", [16, 128, 16384], BF16)
    NE = 96
    GE_s = dscr("GE_s", [NE, 128, TOK], BF16)

    def is_early(c):
        return (c % 8) < 6

    def eidx(c):
        return (c // 8) * 6 + (c % 8)

    Bh_s = [Buf(f"h_s{i}") for i in range(16)]
    Bh1_s = [Buf(f"h1_s{i}") for i in range(16)]
    Bh1T_s = [Buf(f"h1T_s{i}") for i in range(NBLK)]
    BG_s = [Buf(f"G_s{i}") for i in range(16)]
    BGE = [[Buf(f"GE{c}_{k}") for k in range(NBLK)] for c in range(NE)]
    Bout = [Buf(f"out{i}") for i in range(16)]

    with ExitStack() as es:
        S = Sched(nc, es)

        def sb(stack, name, shape, dt=F32):
            return stack.enter_context(nc.sbuf_tensor("s_" + name, shape, dt))

        psA = es.enter_context(nc.psum_tensor("psA", [128, 2048], F32))
        psB = es.enter_context(nc.psum_tensor("psB", [128, 2048], F32))
        PB = [Buf(f"bank{i}") for i in range(8)]

        def bank(i):
            t = psA if i < 4 else psB
            return t[:, (i % 4) * 512:(i % 4 + 1) * 512]

        bank_rr = [0]

        def next_bank():
            i = bank_rr[0]
            bank_rr[0] = (i + 1) % 8
            return i

        dbg_outs = {}

        def dump(name, slot, ap, shape, dt=F32):
            if not dbg or name in dbg_outs:
                return
            d = nc.dram_tensor("dbg_" + name, shape, dt, kind="ExternalOutput").ap()
            dbg_outs[name] = d
            S.store("sp", d, slot, ap, [Buf("dbg_" + name)])

        cst = sb(es, "cst", [128, 128 + 128 + 2048])
        smt = sb(es, "smt", [128, 40])
        Bcst = Buf("cst")
        dc = S.dsem("const")
        dc1 = S.dsem("const1")
        dc2 = S.dsem("const2")
        S.dma("sp", cst[:], cst_d[:, :], dc, writes=[Bcst])
        S.dma("sp", smt[:], sm_d[:, :], dc, writes=[Bcst])
        Bcst.w = {dc.sem: dc.cnt}
        ident = cst[:, 0:128]
        iota128 = cst[:, 128:256]
        iota16 = cst[:, 256:256 + 2048]

        def layer_norm(pr, T1, T2, sms, g_ap, b_ap, Bgb):
            st = sms.t
            stats = st[0:pr, 0:24]
            mv = st[0:pr, 24:26]
            sd = st[0:pr, 26:27]
            rstd = st[0:pr, 27:28]
            nb = st[0:pr, 28:29]
            z = T1.t[0:pr, :]
            xn = T2.t[0:pr, :]
            for c in range(4):
                S.op("dve", lambda e, c=c: e.bn_stats(out=stats[:, c * 6:(c + 1) * 6],
                                                      in_=z[:, c * 512:(c + 1) * 512]),
                     reads=[T1.buf], writes=[sms.buf])
            S.op("dve", lambda e: e.bn_aggr(out=mv, in_=stats), reads=[sms.buf], writes=[sms.buf])
            S.op("dve", lambda e: e.tensor_scalar(out=sd, in0=mv[:, 1:2], scalar1=EPS, scalar2=None,
                                                  op0=ALU.add), reads=[sms.buf], writes=[sms.buf])
            S.op("act", lambda e: e.activation(out=sd, in_=sd, func=AF.Sqrt),
                 reads=[sms.buf], writes=[sms.buf])
            S.op("dve", lambda e: e.reciprocal(out=rstd, in_=sd), reads=[sms.buf], writes=[sms.buf])
            S.op("dve", lambda e: e.scalar_tensor_tensor(out=nb, in0=mv[:, 0:1], scalar=-1.0, in1=rstd,
                                                         op0=ALU.mult, op1=ALU.mult),
                 reads=[sms.buf], writes=[sms.buf])
            S.op("act", lambda e: e.activation(out=xn, in_=z, func=AF.Identity, scale=rstd, bias=nb),
                 reads=[T1.buf, sms.buf], writes=[T2.buf])
            S.op("dve", lambda e: e.tensor_tensor(out=z, in0=xn, in1=g_ap[0:pr, :], op=ALU.mult),
                 reads=[T2.buf, Bgb], writes=[T1.buf])
            S.op("dve", lambda e: e.tensor_tensor(out=xn, in0=z, in1=b_ap[0:pr, :], op=ALU.add),
                 reads=[T1.buf, Bgb], writes=[T2.buf])

        def transpose_rows(pr, T2, dst3, col0, dst_buf):
            for q in range(4):
                bi = next_bank()
                bk = bank(bi)
                for j in range(4):
                    dk = q * 4 + j
                    S.op("pe", lambda e, dk=dk, j=j: e.transpose(
                        out=bk[:, j * 128:j * 128 + pr], in_=T2.t[0:pr, dk * 128:(dk + 1) * 128],
                        identity=ident[0:pr, 0:pr]),
                        reads=[T2.buf, Bcst], writes=[PB[bi]])
                src = bk.rearrange("p (j c) -> p j c", j=4)[:, :, 0:pr]
                S.op("act", lambda e, q=q, src=src: e.activation(
                    out=dst3[:, q * 4:(q + 1) * 4, col0:col0 + pr], in_=src, func=AF.Copy),
                    reads=[PB[bi]], writes=[dst_buf])

        with ExitStack() as ph:
            lnt = sb(ph, "lnt", [128, 4, D])
            Bln = Buf("lnt")
            for i in range(4):
                S.dma("sp", lnt[:, i, :], lnv[i], dc1, writes=[Bln])
            Bln.w = {dc1.sem: dc1.cnt}
            pwt = sb(ph, "pwt", [128, 2048], BF16)
            Spw = Slot(S, "pwt", pwt)
            S.load("pool", Spw, pwt[:], pw_d[:, :])
            pw4 = pwt[:].rearrange("p (a c o) -> p a c o", a=8, c=2)

            LT = [Slot(S, f"lt{i}", sb(ph, f"lt{i}", [128, D])) for i in range(4)]
            LS = [Slot(S, f"ls{i}", sb(ph, f"ls{i}", [128, 32])) for i in range(2)]
            hTs = [Slot(S, f"hT{i}", sb(ph, f"hT{i}", [128, 16, EXT], BF16)) for i in range(2)]
            aext = Slot(S, "aext", sb(ph, "aext", [128, EXT]))
            tmpA = Slot(S, "tmpA", sb(ph, "tmpA", [128, EXT]))
            tmpB = Slot(S, "tmpB", sb(ph, "tmpB", [128, EXT]))
            pooledT = Slot(S, "pooledT", sb(ph, "pooledT", [128, 8, TBK], BF16))
            ya1T = Slot(S, "ya1T", sb(ph, "ya1T", [128, 8, TBK], BF16))
            yb1T = pooledT
            mT = Slot(S, "mT", sb(ph, "mT", [128, 16, TBK], BF16))
            sg0 = Slot(S, "sg0", sb(ph, "sg0", [128, TBK]))
            sg1 = Slot(S, "sg1", sb(ph, "sg1", [128, TBK]))
            NW = 5
            WR = [Slot(S, f"wr{i}", sb(ph, f"wr{i}", [128, 2048], BF16)) for i in range(NW)]
            WO = [Slot(S, f"wo{i}", sb(ph, f"wo{i}", [128, 8192], BF16)) for i in range(2)]
            wr_i = [0]

            def wload(src_ap):
                s = WR[wr_i[0] % NW]
                wr_i[0] += 1
                S.load("pool", s, s.t[:], src_ap)
                return s

            wo_i = [0]
            wo_bf = dscr("wo_bf", [4, 128, 8192], BF16)
            Bwo_bf = [Buf(f"wo_bf{i}") for i in range(4)]
            for nch in range(4):
                wo = WO[nch % 2]
                S.load("pool", wo, wo.t[:].rearrange("p (a x) -> p a x", a=4),
                       wo_d[nch].rearrange("p (a x) -> p a x", a=4))
                S.store("sp", wo_bf[nch], wo, wo.t[:], [Bwo_bf[nch]])
            pending_tr = []

            def phase_a_ln(k, ti):
                pr = 128 if ti < 4 else 16
                r0 = ti * 128
                T1, T2, sms = LT[(ti % 2) * 2], LT[(ti % 2) * 2 + 1], LS[ti % 2]
                S.load("sp", T1, T1.t[0:pr, :], x_in[k, r0:r0 + pr, :])
                layer_norm(pr, T1, T2, sms, lnt[:, 0, :], lnt[:, 1, :], Bln)
                if ti < 4:
                    gt = k * 4 + ti
                    S.store("sp", h_s[gt * 128:(gt + 1) * 128, :], T2, T2.t[:, :], [Bh_s[gt]])

            def phase_a_tr(k, ti):
                pr = 128 if ti < 4 else 16
                T2 = LT[(ti % 2) * 2 + 1]
                hTn = hTs[k % 2]
                transpose_rows(pr, T2, hTn.t, ti * 128, hTn.buf)

            tail_items = []
            for k in range(NBLK):
                hT = hTs[k % 2]
                if k == 0:
                    for ti in range(5):
                        phase_a_ln(0, ti)
                        phase_a_tr(0, ti)

                for cg in range(8):
                    if cg == 2:
                        for fn in tail_items:
                            fn()
                        tail_items.clear()
                    ws = wload(win[cg])
                    w3 = ws.t[:].rearrange("p (k c) -> p k c", k=16)
                    bi = next_bank()
                    bh = next_bank()
                    for dk in range(16):
                        S.op("pe", lambda e, dk=dk: e.matmul(bank(bi), lhsT=w3[:, dk, :], rhs=hT.t[:, dk, 0:TBK],
                                                             start=(dk == 0), stop=(dk == 15)),
                             reads=[ws.buf, hT.buf], writes=[PB[bi]])
                    for dk in range(16):
                        S.op("pe", lambda e, dk=dk: e.matmul(bank(bh)[:, 0:16], lhsT=w3[:, dk, :],
                                                             rhs=hT.t[:, dk, TBK:EXT],
                                                             start=(dk == 0), stop=(dk == 15)),
                             reads=[ws.buf, hT.buf], writes=[PB[bh]])
                    S.op("act", lambda e: e.activation(out=aext.t[:, 16:EXT], in_=bank(bi), func=AF.Copy),
                         reads=[PB[bi]], writes=[aext.buf])
                    S.op("act", lambda e: e.activation(out=aext.t[:, 0:16], in_=bank(bh)[:, 0:16], func=AF.Copy),
                         reads=[PB[bh]], writes=[aext.buf])
                    g = cg // 2
                    w = 2 << g
                    cur = aext
                    off = 0
                    step = 1
                    tmps = [tmpA, tmpB]
                    ti2 = 0
                    while step < w:
                        nxt = tmps[ti2 % 2]
                        ti2 += 1
                        lo = off + step
                        S.op("dve", lambda e, cur=cur, nxt=nxt, lo=lo, step=step: e.tensor_tensor(
                            out=nxt.t[:, lo:EXT], in0=cur.t[:, lo:EXT], in1=cur.t[:, lo - step:EXT - step],
                            op=ALU.add), reads=[cur.buf], writes=[nxt.buf])
                        cur = nxt
                        off = lo
                        step *= 2
                    S.op("dve", lambda e, cur=cur, w=w, cg=cg: e.scalar_tensor_tensor(
                        out=pooledT.t[:, cg, :], in0=cur.t[:, 16:EXT], scalar=1.0 / w, in1=aext.t[:, 16:EXT],
                        op0=ALU.mult, op1=ALU.subtract), reads=[cur.buf, aext.buf], writes=[pooledT.buf])

                dump("pooledT", pooledT, pooledT.t[:], [128, 8, TBK], BF16)
                dump("hT", hT, hT.t[:], [128, 16, EXT], BF16)
                for g in range(4):
                    for oc in range(2):
                        bi = next_bank()
                        for cc in range(2):
                            S.op("pe", lambda e, cc=cc: e.matmul(bank(bi), lhsT=pw4[:, g * 2 + oc, cc, :],
                                                                 rhs=pooledT.t[:, 2 * g + cc, :],
                                                                 start=(cc == 0), stop=(cc == 1)),
                                 reads=[Spw.buf, pooledT.buf], writes=[PB[bi]])
                        S.op("act", lambda e: e.activation(out=ya1T.t[:, 2 * g + oc, :], in_=bank(bi),
                                                           func=AF.Copy,
                                                           scale=smt[:, 2 * g + oc:2 * g + oc + 1]),
                             reads=[PB[bi], Bcst], writes=[ya1T.buf])

                dump("ya1T", ya1T, ya1T.t[:], [128, 8, TBK], BF16)
                for ci in range(8):
                    wh = wload(win[8 + ci])
                    wb = wload(win[16 + ci])
                    wc = wload(win[24 + ci])
                    bh_m, bh_h, bc_m, bb_m = next_bank(), next_bank(), next_bank(), next_bank()
                    for (ws, bm, bhh) in ((wh, bh_m, bh_h), (wc, bc_m, bh_h), (wb, bb_m, None)):
                        w3 = ws.t[:].rearrange("p (k c) -> p k c", k=16)
                        for dk in range(16):
                            S.op("pe", lambda e, dk=dk, w3=w3, bm=bm: e.matmul(
                                bank(bm), lhsT=w3[:, dk, :], rhs=hT.t[:, dk, 0:TBK],
                                start=(dk == 0), stop=(dk == 15)),
                                reads=[ws.buf, hT.buf], writes=[PB[bm]])
                        if bhh is not None:
                            o = 0 if ws is wh else 16
                            for dk in range(16):
                                S.op("pe", lambda e, dk=dk, w3=w3, o=o: e.matmul(
                                    bank(bhh)[:, o:o + 16], lhsT=w3[:, dk, :], rhs=hT.t[:, dk, TBK:EXT],
                                    start=(dk == 0), stop=(dk == 15)),
                                    reads=[ws.buf, hT.buf], writes=[PB[bhh]])
                    S.op("act", lambda e: e.activation(out=tmpA.t[:, 16:EXT], in_=bank(bh_m), func=AF.Copy),
                         reads=[PB[bh_m]], writes=[tmpA.buf])
                    S.op("act", lambda e: e.activation(out=tmpA.t[:, 0:16], in_=bank(bh_h)[:, 0:16], func=AF.Copy),
                         reads=[PB[bh_h]], writes=[tmpA.buf])
                    S.op("dve", lambda e: e.tensor_tensor(out=aext.t[:, 16:EXT], in0=bank(bc_m),
                                                          in1=tmpA.t[:, 16:EXT], op=ALU.mult),
                         reads=[PB[bc_m], tmpA.buf], writes=[aext.buf])
                    S.op("dve", lambda e: e.tensor_tensor(out=aext.t[:, 0:16], in0=bank(bh_h)[:, 16:32],
                                                          in1=tmpA.t[:, 0:16], op=ALU.mult),
                         reads=[PB[bh_h], tmpA.buf], writes=[aext.buf])
                    cw = lambda kk: smt[:, 8 + ci * 3 + kk:8 + ci * 3 + kk + 1]
                    cb = smt[:, 32 + ci:33 + ci]
                    S.op("dve", lambda e: e.tensor_scalar(out=tmpB.t[:, 0:TBK], in0=aext.t[:, 16:EXT],
                                                          scalar1=cw(2), scalar2=cb, op0=ALU.mult, op1=ALU.add),
                         reads=[aext.buf, Bcst], writes=[tmpB.buf])
                    S.op("dve", lambda e: e.scalar_tensor_tensor(out=tmpA.t[:, 0:TBK], in0=aext.t[:, 15:EXT - 1],
                                                                 scalar=cw(1), in1=tmpB.t[:, 0:TBK],
                                                                 op0=ALU.mult, op1=ALU.add),
                         reads=[aext.buf, tmpB.buf, Bcst], writes=[tmpA.buf])
                    S.op("dve", lambda e: e.scalar_tensor_tensor(out=tmpB.t[:, 0:TBK], in0=aext.t[:, 14:EXT - 2],
                                                                 scalar=cw(0), in1=tmpA.t[:, 0:TBK],
                                                                 op0=ALU.mult, op1=ALU.add),
                         reads=[aext.buf, tmpA.buf, Bcst], writes=[tmpB.buf])
                    S.op("dve", lambda e: e.tensor_tensor(out=yb1T.t[:, ci, :], in0=bank(bb_m),
                                                          in1=tmpB.t[:, 0:TBK], op=ALU.mult),
                         reads=[PB[bb_m], tmpB.buf], writes=[yb1T.buf])

                dump("yb1T", yb1T, yb1T.t[:], [128, 8, TBK], BF16)
                for j in range(16):
                    wg0 = wload(win[32 + j])
                    wg1 = wload(win[48 + j])
                    wpc = wload(pcp[j])
                    b0, b1, ba, bb = next_bank(), next_bank(), next_bank(), next_bank()
                    for (ws, bi) in ((wg0, b0), (wg1, b1)):
                        w3 = ws.t[:].rearrange("p (k c) -> p k c", k=16)
                        for dk in range(16):
                            S.op("pe", lambda e, dk=dk, w3=w3, bi=bi: e.matmul(
                                bank(bi), lhsT=w3[:, dk, :], rhs=hT.t[:, dk, 0:TBK],
                                start=(dk == 0), stop=(dk == 15)),
                                reads=[ws.buf, hT.buf], writes=[PB[bi]])
                    w4 = wpc.t[:].rearrange("p (b k c) -> p b k c", b=2, k=8)
                    for (br, bi, src) in ((0, ba, ya1T), (1, bb, yb1T)):
                        for kc in range(8):
                            S.op("pe", lambda e, kc=kc, br=br, bi=bi, src=src: e.matmul(
                                bank(bi), lhsT=w4[:, br, kc, :], rhs=src.t[:, kc, :],
                                start=(kc == 0), stop=(kc == 7)),
                                reads=[wpc.buf, src.buf], writes=[PB[bi]])
                    S.op("act", lambda e: e.activation(out=sg0.t[:], in_=bank(b0), func=AF.Sigmoid),
                         reads=[PB[b0]], writes=[sg0.buf])
                    S.op("act", lambda e: e.activation(out=sg1.t[:], in_=bank(b1), func=AF.Sigmoid),
                         reads=[PB[b1]], writes=[sg1.buf])
                    S.op("dve", lambda e: e.tensor_tensor(out=sg0.t[:], in0=bank(ba), in1=sg0.t[:], op=ALU.mult),
                         reads=[PB[ba], sg0.buf], writes=[sg0.buf])
                    S.op("dve", lambda e: e.tensor_tensor(out=sg1.t[:], in0=bank(bb), in1=sg1.t[:], op=ALU.mult),
                         reads=[PB[bb], sg1.buf], writes=[sg1.buf])
                    S.op("dve", lambda e, j=j: e.tensor_tensor(out=mT.t[:, j, :], in0=sg0.t[:], in1=sg1.t[:],
                                                               op=ALU.add),
                         reads=[sg0.buf, sg1.buf], writes=[mT.buf])
                    if k + 1 < NBLK:
                        if j % 3 == 1:
                            phase_a_ln(k + 1, j // 3)
                        elif j % 3 == 0 and j > 0:
                            phase_a_tr(k + 1, j // 3 - 1)

                dump("mT", mT, mT.t[:], [128, 16, TBK], BF16)
                for pair in range(2):
                    for tp in range(2):
                        tl = pair * 2 + tp
                        gt = k * 4 + tl
                        T1 = LT[(tl % 2) * 2]
                        S.load("sp", T1, T1.t[:, :], h_s[gt * 128:(gt + 1) * 128, :], reads=[Bh_s[gt]])
                    for nch in range(4):
                        wo = WO[wo_i[0] % 2]
                        wo_i[0] += 1
                        S.load("pool", wo, wo.t[:], wo_bf[nch], reads=[Bwo_bf[nch]])
                        wo3 = wo.t[:].rearrange("p (k c) -> p k c", k=16)
                        for tp in range(2):
                            tl = pair * 2 + tp
                            bi = tp * 4 + nch
                            for dk in range(16):
                                S.op("pe", lambda e, dk=dk, tl=tl, bi=bi: e.matmul(
                                    bank(bi), lhsT=mT.t[:, dk, tl * 128:(tl + 1) * 128], rhs=wo3[:, dk, :],
                                    start=(dk == 0), stop=(dk == 15)),
                                    reads=[mT.buf, wo.buf], writes=[PB[bi]])
                    for tp in range(2):
                        tl = pair * 2 + tp
                        gt = k * 4 + tl
                        T1, T2, sms = LT[(tl % 2) * 2], LT[(tl % 2) * 2 + 1], LS[tl % 2]
                        pst = psA if tp == 0 else psB
                        S.op("dve", lambda e: e.scalar_tensor_tensor(
                            out=T1.t[:, :], in0=T1.t[:, :], scalar=ALPHA, in1=pst[:, :],
                            op0=ALU.mult, op1=ALU.add),
                            reads=[T1.buf] + PB[tp * 4:tp * 4 + 4], writes=[T1.buf])
                    for fn in pending_tr:
                        fn()
                    pending_tr.clear()
                    for tp in range(2):
                        tl = pair * 2 + tp
                        gt = k * 4 + tl
                        T1, T2, sms = LT[(tl % 2) * 2], LT[(tl % 2) * 2 + 1], LS[tl % 2]
                        layer_norm(128, T1, T2, sms, lnt[:, 2, :], lnt[:, 3, :], Bln)
                        S.store("sp", h1_s[gt * 128:(gt + 1) * 128, :], T2, T2.t[:, :], [Bh1_s[gt]])
                        pending_tr.append(lambda T2=T2, tl=tl, hT=hT: transpose_rows(128, T2, hT.t, tl * 128, hT.buf))
                tail_items.extend(pending_tr)
                pending_tr.clear()
                tail_items.append(lambda k=k, hT=hT: S.store(
                    "sp", h1T_s[k].rearrange("p (k t) -> p k t", k=16), hT, hT.t[:, :, 0:TBK], [Bh1T_s[k]]))
                if k == NBLK - 1:
                    for fn in tail_items:
                        fn()
                    tail_items.clear()

        if stage <= 1:
            S.barrier(engines=("sp",))
            return nc

        S.barrier()
        with ExitStack() as ph:
            ktt = sb(ph, "ktt", [128, 2048], BF16)
            Skt = Slot(S, "ktt", ktt)
            S.load("pool", Skt, ktt[:], kt_d[:, :])
            kt3 = ktt[:].rearrange("p (g n) -> p g n", g=16)
            H1 = [Slot(S, "h1b0", sb(ph, "h1b0", [128, 16, TBK], BF16))] * 2
            GST = [Slot(S, f"gst{i}", sb(ph, f"gst{i}", [128, TBK], BF16)) for i in range(3)]
            qT = Slot(S, "qT", sb(ph, "qT", [128, 16, TBK], BF16))
            NW = 6
            WR = [Slot(S, f"wq{i}", sb(ph, f"wq{i}", [128, 2048], BF16)) for i in range(NW)]
            Ssc = Slot(S, "Ssc", sb(ph, "Ssc", [128, 2048]))
            Ss2 = Slot(S, "Ss2", sb(ph, "Ss2", [128, 2048]))
            Vt = Slot(S, "Vt", sb(ph, "Vt", [128, 256]))
            It = Slot(S, "It", sb(ph, "It", [128, 256], U32))
            Ift = Slot(S, "Ift", sb(ph, "Ift", [128, 256]))
            Ct = Slot(S, "Ct", sb(ph, "Ct", [128, 2048]))
            C2t = Slot(S, "C2t", sb(ph, "C2t", [128, 2048]))
            Tt = Slot(S, "Tt", sb(ph, "Tt", [128, 128]))
            Jt = Slot(S, "Jt", sb(ph, "Jt", [128, 128], U32))
            thr = Slot(S, "thr", sb(ph, "thr", [128, 2048]))
            S.op("dve", lambda e: e.tensor_scalar(out=thr.t[:], in0=iota16, scalar1=16.0, scalar2=16.0,
                                                  op0=ALU.mult, op1=ALU.add), reads=[Bcst], writes=[thr.buf])
            Kf = Slot(S, "Kf", sb(ph, "Kf", [128, 2, 128]))
            Et = Slot(S, "Et", sb(ph, "Et", [128, 2048]))
            abg = Slot(S, "abg", sb(ph, "abg", [128, 3, 128]))
            gsm = Slot(S, "gsm", sb(ph, "gsm", [128, 16]))
            ABG = [Slot(S, f"ABG{i}", sb(ph, f"ABG{i}", [128, 4, 128])) for i in range(2)]
            DAB = [Slot(S, f"dab{i}", sb(ph, f"dab{i}", [128, 128], BF16)) for i in range(4)]
            NOH = 12
            OA = [Slot(S, f"oa{i}", sb(ph, f"oa{i}", [128, 128], BF16)) for i in range(NOH)]
            OB = [Slot(S, f"ob{i}", sb(ph, f"ob{i}", [128, 128], BF16)) for i in range(NOH)]
            Gb = [Slot(S, f"Gb{i}", sb(ph, f"Gb{i}", [128, 128, 128], BF16)) for i in range(2)]
            iob = Slot(S, "iob", sb(ph, "iob", [128, 128], BF16))
            S.op("act", lambda e: e.activation(out=iob.t[:], in_=iota128, func=AF.Copy),
                 reads=[Bcst], writes=[iob.buf])
            wi = 0
            oh_i = 0
            VG = [Buf(f"VG{i}") for i in range(16)]
            IG = [Buf(f"IG{i}") for i in range(16)]
            S2G = [Buf(f"S2G{i}") for i in range(16)]
            TG = [Buf(f"TG{i}") for i in range(8)]
            JG = [Buf(f"JG{i}") for i in range(8)]
            C2G = [Buf(f"C2G{i}") for i in range(8)]
            def q_block(k):
                nonlocal wi
                hb = H1[k % 2]
                if k == 0:
                    S.load("sp", hb, hb.t[:].rearrange("p k t -> p (k t)"), h1T_s[k], reads=[Bh1T_s[k]])
                for cg in range(16):
                    ws = WR[wi % NW]
                    wi += 1
                    S.load("pool", ws, ws.t[:], wq_d[cg])
                    w3 = ws.t[:].rearrange("p (k c) -> p k c", k=16)
                    bi = wi % 4
                    for dk in range(16):
                        S.op("pe", lambda e, dk=dk: e.matmul(bank(bi), lhsT=w3[:, dk, :], rhs=hb.t[:, dk, :],
                                                             start=(dk == 0), stop=(dk == 15)),
                             reads=[ws.buf, hb.buf], writes=[PB[bi]])
                    S.op("act", lambda e, cg=cg: e.activation(out=qT.t[:, cg, :], in_=bank(bi), func=AF.Copy),
                         reads=[PB[bi]], writes=[qT.buf])
                    yield

            def topk_gen(k, tl):
                hb = H1[k % 2]
                if tl == 0:
                    yield from q_block(k)
                gt = k * 4 + tl
                for g in range(16):
                    S.op("pe", lambda e, g=g: e.matmul(psA[:, g * 128:(g + 1) * 128],
                                                       lhsT=qT.t[:, g, tl * 128:(tl + 1) * 128],
                                                       rhs=kt3[:, g, :], start=True, stop=True),
                         reads=[qT.buf, Skt.buf], writes=[PB[g // 4]])
                S.op("act", lambda e: e.activation(out=Ssc.t[:], in_=psA[:, :], func=AF.Copy),
                     reads=PB[0:4], writes=[Ssc.buf])
                s3 = Ssc.t[:].rearrange("p (g n) -> p g n", g=16)
                s23 = Ss2.t[:].rearrange("p (g n) -> p g n", g=16)
                v3 = Vt.t[:].rearrange("p (g k) -> p g k", g=16)
                i3 = It.t[:].rearrange("p (g k) -> p g k", g=16)
                for ph_ in range(5):
                    for g in range(16):
                        if ph_ == 0:
                            S.op("dve", lambda e, g=g: e.max(out=v3[:, g, 0:8], in_=s3[:, g, :]),
                                 reads=[Ssc.buf], writes=[VG[g]])
                        elif ph_ == 1:
                            S.op("dve", lambda e, g=g: e.max_index(out=i3[:, g, 0:8], in_max=v3[:, g, 0:8],
                                                                   in_values=s3[:, g, :]),
                                 reads=[Ssc.buf, VG[g]], writes=[IG[g]])
                        elif ph_ == 2:
                            S.op("dve", lambda e, g=g: e.match_replace(out=s23[:, g, :],
                                                                       in_to_replace=v3[:, g, 0:8],
                                                                       in_values=s3[:, g, :], imm_value=NEG),
                                 reads=[Ssc.buf, VG[g]], writes=[S2G[g]])
                        elif ph_ == 3:
                            S.op("dve", lambda e, g=g: e.max(out=v3[:, g, 8:16], in_=s23[:, g, :]),
                                 reads=[S2G[g]], writes=[VG[g]])
                        else:
                            S.op("dve", lambda e, g=g: e.max_index(out=i3[:, g, 8:16], in_max=v3[:, g, 8:16],
                                                                   in_values=s23[:, g, :]),
                                 reads=[S2G[g], VG[g]], writes=[IG[g]])
                        if g % 8 == 7:
                            yield
                S.op("dve", lambda e: e.tensor_copy(out=Ift.t[:], in_=It.t[:]),
                     reads=IG, writes=[Ift.buf])
                v4 = Vt.t[:].rearrange("p (h s k) -> p h s k", h=8, s=2)
                if4 = Ift.t[:].rearrange("p (h s k) -> p h s k", h=8, s=2)
                c4 = Ct.t[:].rearrange("p (h a b) -> p h a b", h=8, a=16)
                c24 = C2t.t[:].rearrange("p (h a b) -> p h a b", h=8, a=16)
                S.op("dve", lambda e: e.tensor_tensor(
                    out=c4, in0=v4[:, :, 0, :].unsqueeze(3).broadcast_to([128, 8, 16, 16]),
                    in1=v4[:, :, 1, :].unsqueeze(2).broadcast_to([128, 8, 16, 16]), op=ALU.add),
                    reads=VG, writes=[Ct.buf])
                c3 = Ct.t[:].rearrange("p (h x) -> p h x", h=8)
                c23 = C2t.t[:].rearrange("p (h x) -> p h x", h=8)
                t3 = Tt.t[:].rearrange("p (h k) -> p h k", h=8)
                j3 = Jt.t[:].rearrange("p (h k) -> p h k", h=8)
                for ph_ in range(5):
                    for h in range(8):
                        if ph_ == 0:
                            S.op("dve", lambda e, h=h: e.max(out=t3[:, h, 0:8], in_=c3[:, h, :]),
                                 reads=[Ct.buf], writes=[TG[h]])
                        elif ph_ == 1:
                            S.op("dve", lambda e, h=h: e.max_index(out=j3[:, h, 0:8], in_max=t3[:, h, 0:8],
                                                                   in_values=c3[:, h, :]),
                                 reads=[Ct.buf, TG[h]], writes=[JG[h]])
                        elif ph_ == 2:
                            S.op("dve", lambda e, h=h: e.match_replace(out=c23[:, h, :],
                                                                       in_to_replace=t3[:, h, 0:8],
                                                                       in_values=c3[:, h, :], imm_value=NEG),
                                 reads=[Ct.buf, TG[h]], writes=[C2G[h]])
                        elif ph_ == 3:
                            S.op("dve", lambda e, h=h: e.max(out=t3[:, h, 8:16], in_=c23[:, h, :]),
                                 reads=[C2G[h]], writes=[TG[h]])
                        else:
                            S.op("dve", lambda e, h=h: e.max_index(out=j3[:, h, 8:16], in_max=t3[:, h, 8:16],
                                                                   in_values=c23[:, h, :]),
                                 reads=[C2G[h], TG[h]], writes=[JG[h]])
                    yield
                S.op("dve", lambda e: e.tensor_copy(out=Kf.t[:, 1, :], in_=Jt.t[:]),
                     reads=JG, writes=[Kf.buf])
                S.op("dve", lambda e: e.tensor_tensor(
                    out=Et.t[:].rearrange("p (x j) -> p x j", j=16),
                    in0=Kf.t[:, 1, :].unsqueeze(2).broadcast_to([128, 128, 16]),
                    in1=thr.t[:].rearrange("p (x j) -> p x j", j=16), op=ALU.is_ge),
                    reads=[Kf.buf, thr.buf], writes=[Et.buf])
                S.op("dve", lambda e: e.tensor_reduce(
                    out=Kf.t[:, 0, :], in_=Et.t[:].rearrange("p (x j) -> p x j", j=16),
                    axis=AX.X, op=ALU.add), reads=[Et.buf], writes=[Kf.buf])
                S.op("dve", lambda e: e.scalar_tensor_tensor(
                    out=Kf.t[:, 1, :], in0=Kf.t[:, 0, :], scalar=-16.0, in1=Kf.t[:, 1, :],
                    op0=ALU.mult, op1=ALU.add), reads=[Kf.buf], writes=[Kf.buf])
                e4 = Et.t[:].rearrange("p (h k j) -> p h k j", h=8, k=16)
                io4 = iota16.rearrange("p (h k j) -> p h k j", h=8, k=16)
                for s in range(2):
                    kf3 = Kf.t[:, s, :].rearrange("p (h k) -> p h k", h=8)
                    S.op("dve", lambda e, kf3=kf3: e.tensor_tensor(
                        out=e4, in0=kf3.unsqueeze(3).broadcast_to([128, 8, 16, 16]), in1=io4,
                        op=ALU.is_equal), reads=[Kf.buf, Bcst], writes=[Et.buf])
                    S.op("dve", lambda e, s=s: e.tensor_tensor(
                        out=e4, in0=e4, in1=if4[:, :, s, :].unsqueeze(2).broadcast_to([128, 8, 16, 16]),
                        op=ALU.mult), reads=[Et.buf, Ift.buf], writes=[Et.buf])
                    S.op("dve", lambda e, s=s: e.tensor_reduce(
                        out=abg.t[:, s, :], in_=Et.t[:].rearrange("p (x j) -> p x j", j=16),
                        axis=AX.X, op=ALU.add), reads=[Et.buf], writes=[abg.buf])
                g3 = abg.t[:, 2, :].rearrange("p (h k) -> p h k", h=8)
                S.op("dve", lambda e: e.tensor_tensor(
                    out=g3, in0=t3, in1=t3[:, :, 0:1].broadcast_to([128, 8, 16]), op=ALU.subtract),
                    reads=TG, writes=[abg.buf])
                S.op("act", lambda e: e.activation(out=abg.t[:, 2, :], in_=abg.t[:, 2, :], func=AF.Exp),
                     reads=[abg.buf], writes=[abg.buf])
                S.op("dve", lambda e: e.tensor_reduce(out=gsm.t[:, 0:8], in_=g3, axis=AX.X, op=ALU.add),
                     reads=[abg.buf], writes=[gsm.buf])
                S.op("dve", lambda e: e.reciprocal(out=gsm.t[:, 8:16], in_=gsm.t[:, 0:8]),
                     reads=[gsm.buf], writes=[gsm.buf])
                S.op("dve", lambda e: e.tensor_tensor(
                    out=g3, in0=g3, in1=gsm.t[:, 8:16].unsqueeze(2).broadcast_to([128, 8, 16]),
                    op=ALU.mult), reads=[abg.buf, gsm.buf], writes=[abg.buf])
                AB = ABG[gt % 2]
                bi = 0
                for s in range(3):
                    S.op("pe", lambda e, s=s: e.transpose(out=bank(bi)[:, s * 128:(s + 1) * 128],
                                                          in_=abg.t[:, s, :], identity=ident),
                         reads=[abg.buf, Bcst], writes=[PB[bi]])
                S.op("act", lambda e: e.activation(out=AB.t[:, 0:3, :].rearrange("p s t -> p (s t)"),
                                                   in_=bank(bi)[:, 0:384], func=AF.Copy),
                     reads=[PB[bi]], writes=[AB.buf])
                S.op("dve", lambda e: e.tensor_scalar(out=AB.t[:, 3, :], in0=AB.t[:, 1, :], scalar1=-1.0,
                                                      scalar2=None, op0=ALU.mult),
                     reads=[AB.buf], writes=[AB.buf])

            def onehot_gen(gt):
                nonlocal oh_i
                AB = ABG[gt % 2]
                G = Gb[gt % 2]
                pend_ev = None
                pend_mm = None
                for t4 in range(32):
                    bi = 6 + (t4 % 2)
                    grp_ = []
                    for tt in range(4):
                        t = t4 * 4 + tt
                        grp_.append((tt, t, OA[oh_i % NOH], OB[oh_i % NOH], DAB[oh_i % 4]))
                        oh_i += 1
                    for (tt, t, oa, ob, da) in grp_:
                        S.op("dve", lambda e, t=t, oa=oa: e.tensor_scalar(
                            out=oa.t[:], in0=iob.t[:], scalar1=AB.t[:, 0, t:t + 1], scalar2=AB.t[:, 2, t:t + 1],
                            op0=ALU.is_equal, op1=ALU.mult), reads=[AB.buf, iob.buf], writes=[oa.buf])
                    for (tt, t, oa, ob, da) in grp_:
                        if tt == 0:
                            S.op("act", lambda e, t=t, da=da: e.activation(
                                out=da.t[:], in_=iob.t[:], func=AF.Abs, bias=AB.t[:, 3, t:t + 1], scale=1.0),
                                reads=[AB.buf, iob.buf], writes=[da.buf])
                        else:
                            S.op("dve", lambda e, t=t, ob=ob: e.tensor_scalar(
                                out=ob.t[:], in0=iob.t[:], scalar1=AB.t[:, 1, t:t + 1], scalar2=None,
                                op0=ALU.is_equal), reads=[AB.buf, iob.buf], writes=[ob.buf])
                    for (tt, t, oa, ob, da) in grp_:
                        if tt == 0:
                            S.op("act", lambda e, ob=ob, da=da: e.activation(
                                out=ob.t[:], in_=da.t[:], func=AF.Relu, bias=1.0, scale=-1.0),
                                reads=[da.buf], writes=[ob.buf])
                    def mm_ev(t4=t4, bi=bi, G=G, grp_=grp_):
                        for (tt, t, oa, ob, da) in grp_:
                            S.op("pe", lambda e, tt=tt, oa=oa, ob=ob: e.matmul(
                                bank(bi).rearrange("p (i t) -> p t i", t=4)[:, tt, :], lhsT=ob.t[:], rhs=oa.t[:],
                                start=True, stop=True), reads=[oa.buf, ob.buf], writes=[PB[bi]])

                        def ev():
                            S.op("act", lambda e: e.activation(
                                out=G.t[:, :, t4 * 4:(t4 + 1) * 4],
                                in_=bank(bi).rearrange("p (i t) -> p i t", t=4), func=AF.Copy),
                                reads=[PB[bi]], writes=[G.buf])
                        return ev
                    if pend_mm is not None:
                        new_ev = pend_mm()
                        if pend_ev is not None:
                            pend_ev()
                        pend_ev = new_ev
                    pend_mm = mm_ev
                    yield
                new_ev = pend_mm()
                if pend_ev is not None:
                    pend_ev()
                new_ev()
                S.store("sp", G_s[gt], G, G.t[:].rearrange("p i t -> p (i t)"), [BG_s[gt]])


            def drive2(ga, gb):
                da = ga is None
                db = gb is None
                while not (da and db):
                    if not da:
                        try:
                            next(ga)
                        except StopIteration:
                            da = True
                    if not db:
                        try:
                            next(gb)
                        except StopIteration:
                            db = True

            early_chunks = [c for c in range(128) if is_early(c)]

            def early_gen(k, chunks, prefetch_next=False):
                nonlocal wi
                hb = H1[0]
                for c in chunks:
                    ws = WR[wi % NW]
                    wi += 1
                    S.load("pool", ws, ws.t[:], ut_d[c])
                    u3 = ws.t[:].rearrange("p (k e) -> p k e", k=16)
                    bi = 4 + (c % 2)
                    for dk in range(16):
                        S.op("pe", lambda e, dk=dk: e.matmul(bank(bi), lhsT=u3[:, dk, :], rhs=hb.t[:, dk, :],
                                                             start=(dk == 0), stop=(dk == 15)),
                             reads=[ws.buf, hb.buf], writes=[PB[bi]])
                        if dk % 4 == 3:
                            yield
                    st = GST[eidx(c) % 3]
                    S.op("act", lambda e: e.activation(out=st.t[:], in_=bank(bi), func=AF.Gelu),
                         reads=[PB[bi]], writes=[st.buf])
                    S.store("sp", GE_s[eidx(c), :, k * TBK:(k + 1) * TBK], st, st.t[:], [BGE[eidx(c)][k]])
                if prefetch_next and k + 1 < NBLK:
                    S.load("sp", hb, hb.t[:].rearrange("p k t -> p (k t)"), h1T_s[k + 1], reads=[Bh1T_s[k + 1]])

            def drive3(ga, gb, gc, nc_per_round):
                da = ga is None
                db = gb is None
                dcx = gc is None
                while not (da and db):
                    if not da:
                        try:
                            next(ga)
                        except StopIteration:
                            da = True
                    if not db:
                        try:
                            next(gb)
                        except StopIteration:
                            db = True
                    if not dcx:
                        for _ in range(nc_per_round):
                            try:
                                next(gc)
                            except StopIteration:
                                dcx = True
                                break
                if not dcx:
                    for _ in gc:
                        pass

            tiles = [(k, tl) for k in range(NBLK) for tl in range(4)]
            n4 = (len(early_chunks) + 3) // 4
            eparts = [early_chunks[q * n4:(q + 1) * n4] for q in range(4)]
            drive3(topk_gen(0, 0), None, early_gen(0, eparts[0]), 4)
            for i in range(16):
                k, tl = tiles[i]
                nxt = topk_gen(*tiles[i + 1]) if i + 1 < 16 else None
                eg = None
                if tl < 3:
                    eg = early_gen(k, eparts[tl + 1], prefetch_next=(tl == 2))
                elif k + 1 < NBLK:
                    eg = early_gen(k + 1, eparts[0])
                drive3(onehot_gen(i), nxt, eg, 3)

        if stage <= 2:
            S.barrier(engines=("sp",))
            return nc

        S.barrier()
        with ExitStack() as ph:
            lnt = sb(ph, "lnt2", [128, 2, D])
            Bln = Buf("lnt2")
            for i in range(2):
                S.dma("sp", lnt[:, i, :], lnv[4 + i], dc2, writes=[Bln])
            Bln.w = {dc2.sem: dc2.cnt}
            LS = [Slot(S, f"lv{i}", sb(ph, f"lv{i}", [128, 32])) for i in range(1)] * 2
            h1h = Slot(S, "h1h", sb(ph, "h1h", [128, 2, 16, TBK], BF16))
            acc = [Slot(S, f"acc{i}", sb(ph, f"acc{i}", [128, D])) for i in range(8)]
            CG = 4
            Gg = [Slot(S, f"Gg{i}", sb(ph, f"Gg{i}", [128, 8, CG * 128], BF16)) for i in range(2)]
            NU = 2
            UR = [Slot(S, f"ur{i}", sb(ph, f"ur{i}", [128, 2048], BF16)) for i in range(NU)]
            NV = 8
            vr_all = sb(ph, "vr_all", [128, NV * 2048], BF16)
            VR = [Slot(S, f"vr{i}", vr_all[:, i * 2048:(i + 1) * 2048]) for i in range(NV)]
            LT = [Slot(S, f"lu{i}", vr_all[:, i * 4096:(i + 1) * 4096].bitcast(F32)) for i in range(2)] * 2
            GE = [Slot(S, f"ge{i}", sb(ph, f"ge{i}", [128, 1024], BF16)) for i in range(2)]
            GE4 = [Slot(S, f"gq{i}", sb(ph, f"gq{i}", [128, 1024], BF16)) for i in range(4)]
            CF = [Slot(S, f"cf{i}", sb(ph, f"cf{i}", [128, CG, 1024], BF16)) for i in range(2)]
            ui = 0
            vi = 0
            gi = 0
            for hf in range(2):
                for b2 in range(2):
                    S.load("sp", h1h, h1h.t[:, b2].rearrange("p k t -> p (k t)"), h1T_s[2 * hf + b2],
                           reads=[Bh1T_s[2 * hf + b2]])
                ngrp = 128 // CG
                issued = [0]

                def ensure_loaded(upto):
                    while issued[0] <= min(upto, NE - 1):
                        ei = issued[0]
                        slot = GE4[ei % 4]
                        S.load("sp", slot, slot.t[:], GE_s[ei, :, hf * 1024:(hf + 1) * 1024],
                               reads=[BGE[ei][2 * hf], BGE[ei][2 * hf + 1]])
                        issued[0] += 1

                ensure_loaded(3)

                def load_G(g):
                    Gs_ = Gg[g % 2]
                    S.load("sp", Gs_, Gs_.t[:],
                           G_s[8 * hf:8 * hf + 8, :, g * CG * 128:(g + 1) * CG * 128].rearrange("t p x -> p t x"),
                           reads=BG_s[8 * hf:8 * hf + 8])

                load_G(0)

                def e1(grp):
                    nonlocal ui
                    Gs = Gg[grp % 2]
                    cf = CF[grp % 2]
                    for c in range(CG):
                        if c == 2 and grp + 1 < ngrp:
                            load_G(grp + 1)
                        cgl = grp * CG + c
                        ge = GE[cgl % 2]
                        if is_early(cgl):
                            ensure_loaded(eidx(cgl) + 3)
                            ge = GE4[eidx(cgl) % 4]
                            S.op("dve", lambda e, c=c: e.tensor_tensor(
                                out=cf.t[:, c, :].rearrange("p (t x) -> p t x", t=8),
                                in0=ge.t[:].rearrange("p (t x) -> p t x", t=8),
                                in1=Gs.t[:, :, c * 128:(c + 1) * 128], op=ALU.mult),
                                reads=[ge.buf, Gs.buf], writes=[cf.buf])
                            yield
                            continue
                        us = UR[ui % NU]
                        ui += 1
                        S.load("pool", us, us.t[:], ut_d[cgl])
                        u3 = us.t[:].rearrange("p (k e) -> p k e", k=16)
                        pb = (cgl % 2) * 2
                        for tc in range(2):
                            for dk in range(16):
                                S.op("pe", lambda e, tc=tc, dk=dk: e.matmul(
                                    bank(pb + tc), lhsT=u3[:, dk, :], rhs=h1h.t[:, tc, dk, :],
                                    start=(dk == 0), stop=(dk == 15)),
                                    reads=[us.buf, h1h.buf], writes=[PB[pb + tc]])
                                if dk % 4 == 3:
                                    yield
                        ge = GE[cgl % 2]
                        S.op("act", lambda e: e.activation(out=ge.t[:], in_=psA[:, pb * 512:(pb + 2) * 512],
                                                           func=AF.Gelu),
                             reads=[PB[pb], PB[pb + 1]], writes=[ge.buf])
                        S.op("dve", lambda e, c=c: e.tensor_tensor(
                            out=cf.t[:, c, :].rearrange("p (t x) -> p t x", t=8),
                            in0=ge.t[:].rearrange("p (t x) -> p t x", t=8),
                            in1=Gs.t[:, :, c * 128:(c + 1) * 128], op=ALU.mult),
                            reads=[ge.buf, Gs.buf], writes=[cf.buf])

                def e2(grp):
                    nonlocal vi
                    cf = CF[grp % 2]
                    vs = []
                    for c in range(CG):
                        v = VR[vi % NV]
                        vi += 1
                        S.load("pool", v, v.t[:], vv_d[grp * CG + c])
                        vs.append(v)
                    for tl in range(8):
                        for dh in range(2):
                            pb = 4 + ((tl * 2 + dh) % 2) * 2
                            for c in range(CG):
                                for n in range(2):
                                    S.op("pe", lambda e, c=c, n=n: e.matmul(
                                        bank(pb + n), lhsT=cf.t[:, c, tl * 128:(tl + 1) * 128],
                                        rhs=vs[c].t[:, dh * 1024 + n * 512:dh * 1024 + (n + 1) * 512],
                                        start=(c == 0), stop=(c == CG - 1)),
                                        reads=[cf.buf, vs[c].buf], writes=[PB[pb + n]])
                            src = psB[:, (pb - 4) * 512:(pb - 4 + 2) * 512]
                            dst = acc[tl].t[:, dh * 1024:(dh + 1) * 1024]
                            if grp == 0:
                                S.op("act", lambda e: e.activation(out=dst, in_=src, func=AF.Copy),
                                     reads=[PB[pb], PB[pb + 1]], writes=[acc[tl].buf])
                            else:
                                S.op("dve", lambda e: e.tensor_tensor(out=dst, in0=src, in1=dst, op=ALU.add),
                                     reads=[PB[pb], PB[pb + 1], acc[tl].buf], writes=[acc[tl].buf])
                            yield

                def drive(ga, gb):
                    da = ga is None
                    db = gb is None
                    while not (da and db):
                        if not db:
                            try:
                                next(gb)
                            except StopIteration:
                                db = True
                        if not da:
                            try:
                                next(ga)
                            except StopIteration:
                                da = True

                drive(e1(0), None)
                for grp in range(ngrp):
                    drive(e1(grp + 1) if grp + 1 < ngrp else None, e2(grp))
                S.barrier()
                for tl in range(8):
                    gt = hf * 8 + tl
                    T1, T2, sms = LT[(tl % 2) * 2], LT[(tl % 2) * 2 + 1], LS[tl % 2]
                    S.load("sp", T2, T2.t[:, :], h1_s[gt * 128:(gt + 1) * 128, :], reads=[Bh1_s[gt]])
                    S.op("dve", lambda e: e.scalar_tensor_tensor(
                        out=T1.t[:, :], in0=T2.t[:, :], scalar=ALPHA, in1=acc[tl].t[:, :],
                        op0=ALU.mult, op1=ALU.add), reads=[T2.buf, acc[tl].buf], writes=[T1.buf])
                    layer_norm(128, T1, T2, sms, lnt[:, 0, :], lnt[:, 1, :], Bln)
                    S.store("sp", out_d[gt * 128:(gt + 1) * 128, :], T2, T2.t[:, :], [Bout[gt]])
                S.barrier()
        S.barrier(engines=("sp",))
    return nc


def prep_shared(inp):
    f = np.float32
    L = 0
    sh = {}
    vecs = [inp["ln_in_g"], inp["ln_in_b"], inp["ln1_g"][L], inp["ln1_b"][L], inp["ln2_g"][L], inp["ln2_b"][L]]
    sh["lnv"] = np.ascontiguousarray(np.broadcast_to(np.stack(vecs)[:, None, :], (6, 128, D))).astype(f)
    sm = np.zeros((128, 40), f)
    sm[:, 0:8] = inp["pool_scale"][L].reshape(8, 128).T
    cw = inp["conv_w"][L][:, 0, :]
    sm[:, 8:32] = cw.reshape(3, 8, 128).transpose(2, 1, 0).reshape(128, 24)
    sm[:, 32:40] = inp["conv_b"][L].reshape(8, 128).T
    sh["sm"] = sm
    cst = np.zeros((128, 128 + 128 + 2048), f)
    cst[:, 0:128] = np.eye(128, dtype=f)
    cst[:, 128:256] = np.arange(128, dtype=f)[None, :]
    cst[:, 256:] = (np.arange(2048) % 16).astype(f)[None, :]
    sh["cst"] = cst

    def colgroups(w, kc):
        n = w.shape[1] // 128
        return np.ascontiguousarray(w.reshape(kc, 128, n, 128).transpose(2, 1, 0, 3).reshape(n, 128, kc * 128))
    sh["win"] = colgroups(inp["w_in"][L], 16)
    pwm = inp["pool_w"][L]
    sh["pw"] = np.ascontiguousarray(
        pwm.reshape(4, 2, 128, 2, 128).transpose(2, 0, 3, 1, 4).reshape(128, 2048))
    pp = colgroups(inp["pool_proj"][L], 8)
    cp = colgroups(inp["conv_proj"][L], 8)
    sh["pcp"] = np.ascontiguousarray(np.concatenate([pp, cp], axis=2))
    wo = inp["w_out"][L]
    sh["wo"] = np.ascontiguousarray(wo.reshape(16, 128, 4, 512).transpose(2, 1, 0, 3).reshape(4, 128, 8192))
    sh["wq"] = colgroups(inp["peer_wq"][L], 16)
    k1 = inp["peer_keys_1"][L]
    k2 = inp["peer_keys_2"][L]
    kt = np.stack([k1, k2], axis=1)
    sh["kt"] = np.ascontiguousarray(kt.transpose(3, 0, 1, 2).reshape(128, 2048))
    U = inp["expert_u"][L]
    sh["ut"] = np.ascontiguousarray(U.reshape(128, 128, 16, 128).transpose(0, 3, 2, 1).reshape(128, 128, 2048))
    sh["vv"] = np.ascontiguousarray(inp["expert_v"][L].reshape(128, 128, 2048))
    return sh


def prep_x(inp):
    x = inp["x"]
    meta = inp["meta_tokens"]
    xs = []
    for c in range(NCORE):
        b, j = divmod(c, 4)
        full = np.concatenate([meta, x[b]], axis=0)
        blk = np.empty((NBLK, EXT, D), np.float32)
        for k in range(NBLK):
            p0 = 16 + 2048 * j + TBK * k
            blk[k, 0:TBK] = full[p0:p0 + TBK]
            blk[k, TBK:EXT] = full[p0 - 16:p0]
        xs.append(blk)
    return xs


_NC_CACHE = {}


def kernel(**inputs):
    inp = {k: np.asarray(v) for k, v in inputs.items()}
    sh = prep_shared(inp)
    xs = prep_x(inp)
    if "nc" not in _NC_CACHE:
        _NC_CACHE["nc"] = build_program()
    nc = _NC_CACHE["nc"]
    in_maps = []
    for c in range(NCORE):
        m = dict(sh)
        m["x_in"] = xs[c]
        in_maps.append(m)
    res = run_bass_kernel_spmd(nc, in_maps, core_ids=list(range(NCORE)))
    out = np.empty((2, 8192, D), np.float32)
    for c in range(NCORE):
        b, j = divmod(c, 4)
        out[b, 2048 * j:2048 * (j + 1)] = np.asarray(res.results[c]["out"])
    return out
```
